# Optimizing a Trainium2 kernel written in Bass

```python
import jax, jax.numpy as jnp
from jax import lax
import numpy as np

D_MODEL = 1024
BATCH = 16
SEQ = 2048
DEPTH = 4

N_MIXERS = 2
N_META = 16
N_HEADS = 16
QK_NOPE_DIM = 64
QK_ROPE_DIM = 32
QK_HEAD_DIM = QK_NOPE_DIM + QK_ROPE_DIM
V_HEAD_DIM = 64
Q_LORA_RANK = 384
KV_LORA_RANK = 256
ROPE_THETA = 10000.0
Q_BLOCK = 128
POOL_WINDOWS = (2, 4, 8, 16)
N_POOL_GROUPS = len(POOL_WINDOWS)
POOL_GROUP_DIM = D_MODEL // N_POOL_GROUPS
D_FF = 2816
CONV_WIDTH = 3
NORM_EPS = 1e-6
N_MLA_LAYERS = len(range(0, DEPTH, N_MIXERS))
N_POOL_LAYERS = DEPTH - N_MLA_LAYERS

kernel_name = "hybrid_mla_multiscale_pool_convffn"


def rmsnorm(x, g):
    xf = x.astype(jnp.float32)
    y = xf * lax.rsqrt(jnp.mean(xf * xf, axis=-1, keepdims=True) + NORM_EPS)
    return (y * g.astype(jnp.float32)).astype(x.dtype)


def rope_tables(length):
    inv = 1.0 / (ROPE_THETA ** (jnp.arange(0, QK_ROPE_DIM, 2, dtype=jnp.float32) / QK_ROPE_DIM))
    ang = jnp.arange(length, dtype=jnp.float32)[:, None] * inv[None, :]
    return jnp.cos(ang), jnp.sin(ang)


def apply_rope(x, cos, sin):
    xf = x.astype(jnp.float32)
    x1, x2 = jnp.split(xf, 2, axis=-1)
    c = cos[None, :, None, :]
    s = sin[None, :, None, :]
    return jnp.concatenate([x1 * c - x2 * s, x2 * c + x1 * s], axis=-1).astype(x.dtype)


def mla_mixer(h, w_dqkv, q_norm, w_uq, kv_norm, w_ukv, w_o, cos, sin):
    B, L, _ = h.shape
    a = h @ w_dqkv
    c_q, c_kv, k_rope = jnp.split(a, [Q_LORA_RANK, Q_LORA_RANK + KV_LORA_RANK], axis=-1)
    c_q = rmsnorm(c_q, q_norm)
    c_kv = rmsnorm(c_kv, kv_norm)
    q = (c_q @ w_uq).reshape(B, L, N_HEADS, QK_HEAD_DIM)
    q = jnp.concatenate([q[..., :QK_NOPE_DIM], apply_rope(q[..., QK_NOPE_DIM:], cos, sin)], axis=-1)
    kv = (c_kv @ w_ukv).reshape(B, L, N_HEADS, QK_NOPE_DIM + V_HEAD_DIM)
    k_nope, v = jnp.split(kv, [QK_NOPE_DIM], axis=-1)
    k_rope = apply_rope(k_rope[:, :, None, :], cos, sin)
    k = jnp.concatenate([k_nope, jnp.broadcast_to(k_rope, (B, L, N_HEADS, QK_ROPE_DIM))], axis=-1)
    scale = QK_HEAD_DIM ** -0.5
    outs = []
    for start in range(0, L, Q_BLOCK):
        end = start + Q_BLOCK
        qb = q[:, start:end]
        kb = k[:, :end]
        vb = v[:, :end]
        s = jnp.einsum('bqhd,bkhd->bhqk', qb, kb).astype(jnp.float32) * scale
        mask = jnp.arange(start, end)[:, None] >= jnp.arange(end)[None, :]
        s = jnp.where(mask[None, None], s, -jnp.inf)
        p = jax.nn.softmax(s, axis=-1).astype(vb.dtype)
        outs.append(jnp.einsum('bhqk,bkhd->bqhd', p, vb))
    o = jnp.concatenate(outs, axis=1).reshape(B, L, N_HEADS * V_HEAD_DIM)
    return o @ w_o


def pool_mixer(h, w_group, scale):
    B, L, D = h.shape
    hf = h.astype(jnp.float32).reshape(B, L, N_POOL_GROUPS, POOL_GROUP_DIM)
    csum = jnp.cumsum(hf, axis=1)
    t = jnp.arange(1, L + 1, dtype=jnp.float32)
    means = []
    for g, w in enumerate(POOL_WINDOWS):
        cg = csum[:, :, g, :]
        prev = jnp.pad(cg[:, :L - w], ((0, 0), (w, 0), (0, 0)))
        cnt = jnp.minimum(t, float(w))[None, :, None]
        means.append((cg - prev) / cnt)
    pooled = jnp.stack(means, axis=2)
    mixed = (pooled - hf).astype(h.dtype)
    y = jnp.einsum('blgc,gcd->blgd', mixed, w_group).reshape(B, L, D)
    return y * scale


def conv_ffn(h, w_up, conv_w, conv_b, w_down):
    L = h.shape[1]
    u = h @ w_up
    up = jnp.pad(u, ((0, 0), (CONV_WIDTH - 1, 0), (0, 0)))
    u = conv_b + sum(conv_w[j] * up[:, j:j + L] for j in range(CONV_WIDTH))
    gate, val = jnp.split(u, 2, axis=-1)
    return (jax.nn.silu(gate) * val) @ w_down


def setup_inputs(seed: int = 0) -> dict:
    key = jax.random.key(seed)
    ks = jax.random.split(key, 24)
    f32 = jnp.float32

    def nrm(k, shape, fan_in):
        return jax.random.normal(k, shape, f32) * (fan_in ** -0.5)

    def gain(k, shape, s=0.05):
        return 1.0 + s * jax.random.normal(k, shape, f32)

    return {
        "x": jax.random.normal(ks[0], (BATCH, SEQ, D_MODEL), f32),
        "meta_tokens": jax.random.normal(ks[1], (N_META, D_MODEL), f32),
        "norm_mix_pre": gain(ks[2], (DEPTH, D_MODEL)),
        "norm_mix_post": gain(ks[3], (DEPTH, D_MODEL)),
        "norm_ffn_pre": gain(ks[4], (DEPTH, D_MODEL)),
        "norm_ffn_post": gain(ks[5], (DEPTH, D_MODEL)),
        "mla_w_dqkv": nrm(ks[6], (N_MLA_LAYERS, D_MODEL, Q_LORA_RANK + KV_LORA_RANK + QK_ROPE_DIM), D_MODEL),
        "mla_q_norm": gain(ks[7], (N_MLA_LAYERS, Q_LORA_RANK)),
        "mla_w_uq": nrm(ks[8], (N_MLA_LAYERS, Q_LORA_RANK, N_HEADS * QK_HEAD_DIM), Q_LORA_RANK),
        "mla_kv_norm": gain(ks[9], (N_MLA_LAYERS, KV_LORA_RANK)),
        "mla_w_ukv": nrm(ks[10], (N_MLA_LAYERS, KV_LORA_RANK, N_HEADS * (QK_NOPE_DIM + V_HEAD_DIM)), KV_LORA_RANK),
        "mla_w_o": nrm(ks[11], (N_MLA_LAYERS, N_HEADS * V_HEAD_DIM, D_MODEL), N_HEADS * V_HEAD_DIM),
        "pool_w_group": nrm(ks[12], (N_POOL_LAYERS, N_POOL_GROUPS, POOL_GROUP_DIM, POOL_GROUP_DIM), POOL_GROUP_DIM),
        "pool_scale": gain(ks[13], (N_POOL_LAYERS, D_MODEL), 0.1),
        "ffn_w_up": nrm(ks[14], (DEPTH, D_MODEL, 2 * D_FF), D_MODEL),
        "ffn_conv_w": nrm(ks[15], (DEPTH, CONV_WIDTH, 2 * D_FF), CONV_WIDTH),
        "ffn_conv_b": 0.01 * jax.random.normal(ks[16], (DEPTH, 2 * D_FF), f32),
        "ffn_w_down": nrm(ks[17], (DEPTH, D_FF, D_MODEL), D_FF),
    }


def reference(x, meta_tokens, norm_mix_pre, norm_mix_post, norm_ffn_pre, norm_ffn_post,
              mla_w_dqkv, mla_q_norm, mla_w_uq, mla_kv_norm, mla_w_ukv, mla_w_o,
              pool_w_group, pool_scale, ffn_w_up, ffn_conv_w, ffn_conv_b, ffn_w_down):
    B, S, D = x.shape
    L = N_META + S
    L_pad = -(-L // Q_BLOCK) * Q_BLOCK
    meta = jnp.broadcast_to(meta_tokens.astype(x.dtype)[None], (B, N_META, D))
    h = jnp.concatenate([meta, x, jnp.zeros((B, L_pad - L, D), x.dtype)], axis=1)
    cos, sin = rope_tables(L_pad)
    for i in range(DEPTH):
        j = i // N_MIXERS
        a = rmsnorm(h, norm_mix_pre[i])
        if i % N_MIXERS == 0:
            m = mla_mixer(a, mla_w_dqkv[j], mla_q_norm[j], mla_w_uq[j], mla_kv_norm[j],
                          mla_w_ukv[j], mla_w_o[j], cos, sin)
        else:
            m = pool_mixer(a, pool_w_group[j], pool_scale[j])
        h = h + rmsnorm(m, norm_mix_post[i])
        f = conv_ffn(rmsnorm(h, norm_ffn_pre[i]), ffn_w_up[i], ffn_conv_w[i], ffn_conv_b[i], ffn_w_down[i])
        h = h + rmsnorm(f, norm_ffn_post[i])
    return h[:, N_META:L]
```

```python
import numpy as np
from contextlib import ExitStack
import concourse.bass as bass
import concourse.mybir as mybir
from concourse.bass_utils import run_bass_kernel_spmd

F32 = mybir.dt.float32
BF16 = mybir.dt.bfloat16
U8 = mybir.dt.uint8
ALU = mybir.AluOpType
AF = mybir.ActivationFunctionType

NCORES = 8
D = 1024
DC = 8
NMETA = 16
SEQ = 2048
LT = NMETA + SEQ
TW = 344
NT = 6
HT = 3
HW = HT * TW
DFF = 2816
FC = 22
NH = 16
QR, KVR, RR = 384, 256, 32
EPS = 1e-6
NKB = 17
ARENA = 135808
SCALE = 96 ** -0.5
NEG = -30000.0

ENGS = ("pe", "act", "dve", "pool", "sp")
import os as _os
STRICT = bool(int(_os.environ.get("MK_STRICT", "0")))
_E = dict(kv.split("=") for kv in _os.environ.get("MK_ENG", "").split(",") if kv)
ENG = {"postmul": "dve", "hid": "dve", "pooladd": "dve", "krope": "act", "ropeadd": "pool", "otmul": "pool"}
ENG.update(_E)


class TT:
    __slots__ = ("w", "r")

    def __init__(self):
        self.w = None
        self.r = []


class Op:
    __slots__ = ("eng", "emit", "deps", "inc", "sem", "val")

    def __init__(self, eng, emit, deps, inc, sem):
        self.eng, self.emit, self.deps, self.inc, self.sem, self.val = eng, emit, deps, inc, sem, None


class Prog:
    def __init__(self, nc):
        self.nc = nc
        self.ops = {e: [] for e in ENGS}
        self.n_dma_sem = 0

    def new_dma_sem(self, name=None):
        if name is not None:
            if not hasattr(self, "_named"):
                self._named = {}
            if name not in self._named:
                self.n_dma_sem += 1
                self._named[name] = ("dma", self.n_dma_sem)
            return self._named[name]
        self.n_dma_sem += 1
        return ("dma", self.n_dma_sem)

    def add(self, eng, emit, reads=(), writes=(), dma_sem=None, accum=False):
        deps = []
        is_dma = dma_sem is not None

        def need(d, raw):
            if d is None:
                return
            if (not STRICT) and (not raw) and (not is_dma) and d.sem is None and d.eng == eng:
                return
            deps.append(d)
        for t in reads:
            need(t.w, True)
        if not accum:
            for t in writes:
                need(t.w, False)
                for d in t.r:
                    need(d, False)
        op = Op(eng, emit, deps, True, dma_sem)
        self.ops[eng].append(op)
        for t in reads:
            t.r.append(op)
        for t in writes:
            t.w = op
            t.r = []
        return op

    def finalize(self, stack):
        nc = self.nc
        eng_sem = {e: stack.enter_context(nc.semaphore("c_" + e)) for e in ENGS}
        dsem, dcount = {}, {}
        for e in ENGS:
            c = 0
            for op in self.ops[e]:
                if op.sem is not None:
                    if op.sem not in dsem:
                        dsem[op.sem] = stack.enter_context(nc.semaphore("d%d" % op.sem[1]))
                        dcount[op.sem] = 0
                    dcount[op.sem] += 16
                    op.val = dcount[op.sem]
                else:
                    c += 1
                    op.val = c
        engobj = {"pe": "tensor", "act": "scalar", "dve": "vector", "pool": "gpsimd", "sp": "sync"}
        with nc.Block() as block:
            def run(e, eng):
                seen = {}
                for op in self.ops[e]:
                    need = {}
                    for d in op.deps:
                        s = dsem[d.sem] if d.sem is not None else eng_sem[d.eng]
                        key = id(s)
                        if seen.get(key, 0) >= d.val:
                            continue
                        if key not in need or need[key][1] < d.val:
                            need[key] = (s, d.val)
                    for key, (s, v) in need.items():
                        eng.wait_ge(s, v)
                        seen[key] = v
                    ins = op.emit(eng)
                    if op.sem is not None:
                        ins.then_inc(dsem[op.sem], 16)
                    else:
                        ins.then_inc(eng_sem[e], 1)
                fin = {}
                for op in self.ops[e]:
                    if op.sem is not None:
                        fin[op.sem] = max(fin.get(op.sem, 0), op.val)
                for k, v in fin.items():
                    if seen.get(id(dsem[k]), 0) < v:
                        eng.wait_ge(dsem[k], v)

            for e in ENGS:
                getattr(block, engobj[e])(lambda eng, e=e: run(e, eng))


class MK:
    def __init__(self, nc, cfg, st):
        self.nc, self.cfg, self.st = nc, cfg, st
        self.P = Prog(nc)
        P = self.P
        dt = nc.dram_tensor
        self.x = dt("x", [2, SEQ, D], F32, kind="ExternalInput").ap()
        self.meta = dt("meta_tokens", [NMETA, D], F32, kind="ExternalInput").ap()
        self.norms = [dt(n, [4, D], F32, kind="ExternalInput").ap()
                      for n in ("norm_mix_pre", "norm_mix_post", "norm_ffn_pre", "norm_ffn_post")]
        self.w_dqkv = dt("mla_w_dqkv", [2, 128, DC * 704], F32, kind="ExternalInput").ap()
        self.q_norm = dt("mla_q_norm", [2, QR], F32, kind="ExternalInput").ap()
        self.w_uq = dt("mla_w_uq", [2, 8, 128, 3 * 192], F32, kind="ExternalInput").ap()
        self.kv_norm = dt("mla_kv_norm", [2, KVR], F32, kind="ExternalInput").ap()
        self.w_ukv = dt("mla_w_ukv", [2, 8, 128, 2 * 256], F32, kind="ExternalInput").ap()
        self.w_o = dt("mla_w_o", [2, 128, DC * D], F32, kind="ExternalInput").ap()
        self.pool_w = dt("pool_w_group", [2, 128, 4 * 2 * 256], F32, kind="ExternalInput").ap()
        self.pool_scale = dt("pool_scale", [2, D], F32, kind="ExternalInput").ap()
        self.w_up = dt("ffn_w_up", [4, FC // 2, 128, 2 * DC * 256], F32, kind="ExternalInput").ap()
        self.conv_w = dt("ffn_conv_w", [4, 3, 2 * DFF], F32, kind="ExternalInput").ap()
        self.conv_b = dt("ffn_conv_b", [4, 2 * DFF], F32, kind="ExternalInput").ap()
        self.w_down = dt("ffn_w_down", [4, 4, 128, FC * 256], F32, kind="ExternalInput").ap()
        self.rope_cs = dt("rope_cs", [2, RR, LT], F32, kind="ExternalInput").ap()
        self.out = dt("out", [2, SEQ, D], F32, kind="ExternalOutput").ap()

        sb = lambda name, shape, d: st.enter_context(nc.sbuf_tensor(name, shape, d))
        self.h = sb("h", [128, DC, LT], F32)
        self.arena = sb("arena", [128, ARENA], U8)
        self.PT = sb("ptab", [128, 896], F32)
        self.ident = sb("ident", [128, 128], F32)
        self.identb = sb("identb", [128, 128], BF16)
        self.onesb = sb("onesb", [128, 128], BF16)
        self.maskb = sb("maskb", [128, 128], BF16)
        self.invc = sb("invc", [128, 4, 16], F32)
        self.ps = [st.enter_context(nc.psum_tensor("ps%d" % i, [128, 512], F32)) for i in range(8)]
        self.ps_tt = [TT() for _ in range(8)]
        self.h_tt = [[TT() for _ in range(NT)] for _ in range(DC)]
        self.t_const = TT()
        self.t_pt = TT()
        self.live = []
        o = ARENA - 20160
        self.sqs = [self.av(o + i * 5760, [DC, 360], BF16) for i in range(2)]
        self.sq_tts = [TT(), TT()]
        self.sq_i = 0
        self.sq, self.sq_tt = self.sqs[0], self.sq_tts[0]
        o += 11520
        self.rs = [self.av(o + i * 1440, [360], F32) for i in range(2)]
        self.rstd = [self.av(o + 2880 + i * 1440, [360], F32) for i in range(2)]
        self.rs_tt = [TT(), TT()]
        self.rstd_tt = [TT(), TT()]
        o += 5760
        self.tpost = [self.av(o + i * 1440, [360], F32) for i in range(2)]
        self.tpost_tt = [TT(), TT()]
        self.PHASE_END = ARENA - 20160
        self.nrm_i = 0
        self.tp_i = 0
        self.rr = {}

    def av(self, off, shape, dtype, p0=0, p1=128):
        esz = 2 if dtype == BF16 else (4 if dtype == F32 else 1)
        n = int(np.prod(shape))
        assert off % 4 == 0 and off + n * esz <= ARENA, (off, n, esz)
        v = self.arena[p0:p1, off:off + n * esz].bitcast(dtype)
        if len(shape) == 2:
            v = v.rearrange("p (a b) -> p a b", a=shape[0])
        elif len(shape) == 3:
            v = v.rearrange("p (a b c) -> p a b c", a=shape[0], b=shape[1])
        return v

    def alloc(self, off, nbytes, n=1):
        assert off + nbytes <= self.PHASE_END, (off, nbytes)
        tts = [TT() for _ in range(n)]
        keep = []
        for (o0, o1, olds) in self.live:
            if o0 < off + nbytes and off < o1:
                for old in olds:
                    for t in tts:
                        if old.w is not None:
                            t.r.append(old.w)
                        t.r.extend(old.r)
                if not (off <= o0 and o1 <= off + nbytes):
                    keep.append((o0, o1, olds))
            else:
                keep.append((o0, o1, olds))
        keep.append((off, off + nbytes, tts))
        self.live = keep
        return tts if n > 1 else tts[0]

    def sq_rot(self):
        self.sq_i += 1
        self.sq, self.sq_tt = self.sqs[self.sq_i % 2], self.sq_tts[self.sq_i % 2]

    def bank(self, group, banks):
        i = self.rr.get(group, 0)
        self.rr[group] = i + 1
        b = banks[i % len(banks)]
        return self.ps[b], self.ps_tt[b]

    def htts(self, c0, c1, t0, t1):
        return [self.h_tt[c][t] for c in range(c0, c1) for t in range(t0 // TW, (t1 - 1) // TW + 1)]

    def mm(self, out, lhsT, rhs, start, stop, reads, writes, tp=None):
        kw = dict(start=start, stop=stop)
        if tp is not None:
            kw["tile_position"] = tp
        self.P.add("pe", lambda e: e.matmul(out, lhsT=lhsT, rhs=rhs, **kw), reads=reads, writes=writes,
                   accum=not start)

    def act(self, out, in_, func, reads, writes, **kw):
        self.P.add("act", lambda e: e.activation(out=out, in_=in_, func=func, **kw), reads=reads, writes=writes)

    def tt_op(self, eng, out, in0, in1, op, reads, writes):
        self.P.add(eng, lambda e: e.tensor_tensor(out=out, in0=in0, in1=in1, op=op), reads=reads, writes=writes)

    def stt(self, eng, out, in0, scalar, in1, op0, op1, reads, writes):
        self.P.add(eng, lambda e: e.scalar_tensor_tensor(out=out, in0=in0, scalar=scalar, in1=in1, op0=op0, op1=op1),
                   reads=reads, writes=writes)

    def copy(self, eng, out, in_, reads, writes):
        if eng == "act":
            self.P.add("act", lambda e: e.copy(out=out, in_=in_), reads=reads, writes=writes)
        else:
            self.P.add(eng, lambda e: e.tensor_copy(out=out, in_=in_), reads=reads, writes=writes)

    def dma(self, eng, out, in_, reads, writes, sem):
        self.P.add(eng, lambda e: e.dma_start(out=out, in_=in_), reads=reads, writes=writes, dma_sem=sem)

    def pcol(self, col):
        return self.PT[:, col:col + 1]

    def setup(self):
        P = self.P
        tc = self.t_const
        ident, identb, onesb, maskb, invc = self.ident, self.identb, self.onesb, self.maskb, self.invc
        P.add("pool", lambda e: e.memset(ident[:], 1.0), writes=[tc])
        P.add("pool", lambda e: e.affine_select(out=ident[:], in_=ident[:], pattern=[[-1, 128]],
                                                 compare_op=ALU.is_equal, fill=0.0, base=0, channel_multiplier=1),
              reads=[tc], writes=[tc])
        P.add("pool", lambda e: e.tensor_copy(out=identb[:], in_=ident[:]), reads=[tc], writes=[tc])
        P.add("pool", lambda e: e.memset(onesb[:], 1.0), reads=[tc], writes=[tc])
        P.add("pool", lambda e: e.memset(maskb[:], 0.0), reads=[tc], writes=[tc])
        P.add("pool", lambda e: e.affine_select(out=maskb[:], in_=maskb[:], pattern=[[1, 128]],
                                                 compare_op=ALU.is_ge, fill=NEG, base=0, channel_multiplier=-1),
              reads=[tc], writes=[tc])
        for t in range(16):
            for g, w in enumerate((2, 4, 8, 16)):
                P.add("pool", lambda e, g=g, t=t, w=w: e.memset(invc[:, g, t:t + 1], 1.0 / min(t + 1, w)),
                      reads=[tc], writes=[tc])
        segs = []
        col = 0
        for n in self.norms:
            segs.append((n.rearrange("l (c p) -> (l c) p", p=128), 32))
        segs.append((self.q_norm.rearrange("j (c p) -> (j c) p", p=128), 6))
        segs.append((self.kv_norm.rearrange("j (c p) -> (j c) p", p=128), 4))
        segs.append((self.pool_scale.rearrange("j (c p) -> (j c) p", p=128), 16))
        segs.append((self.conv_w.rearrange("l t (c p) -> (l t c) p", p=128), 528))
        segs.append((self.conv_b.rearrange("l (c p) -> (l c) p", p=128), 176))
        if self.cfg.get("skip_params"):
            self.G_MIXPRE, self.G_MIXPOST, self.G_FFNPRE, self.G_FFNPOST = 0, 32, 64, 96
            self.G_Q, self.G_KV, self.G_PS, self.G_CW, self.G_CB = 128, 134, 138, 154, 682
            return
        stg = [self.av(i * 512, [128], F32) for i in range(7)]
        stg_tt = self.alloc(0, 7 * 512, n=7)
        r = 0
        for ap, n in segs:
            a = 0
            while a < n:
                ti, ro = divmod(r, 128)
                m = min(n - a, 128 - ro)
                self.dma("sp", stg[ti][ro:ro + m, :], ap[a:a + m, :], [], [stg_tt[ti]], P.new_dma_sem())
                a += m
                r += m
        self.NPAR = r
        assert r == 858
        for ti in range(7):
            rows = min(128, r - ti * 128)
            pb, pt_ = self.bank("setup", [0, 1])
            self.P.add("pe", lambda e, pb=pb, ti=ti, rows=rows: e.transpose(pb[:, 0:rows], stg[ti][0:rows, :], ident[0:rows, 0:rows]),
                       reads=[stg_tt[ti], tc], writes=[pt_])
            self.copy("dve", self.PT[:, ti * 128:ti * 128 + rows], pb[:, 0:rows], [pt_], [self.t_pt])
        self.G_MIXPRE, self.G_MIXPOST, self.G_FFNPRE, self.G_FFNPOST = 0, 32, 64, 96
        self.G_Q, self.G_KV, self.G_PS, self.G_CW, self.G_CB = 128, 134, 138, 154, 682

    def seq_io(self, s_store, s_load):
        P = self.P
        xs = [self.av(i * 4096, [D], F32) for i in range(2)]
        xs_tt = self.alloc(0, 8192, n=2)
        xsem = [P.new_dma_sem('xs0'), P.new_dma_sem('xs1')]
        xm = self.av(8192, [D], F32)
        xm_tt = self.alloc(8192, 4096)
        os_ = [self.av(12288 + i * 4096, [D], F32) for i in range(2)]
        os_tt = self.alloc(12288, 8192, n=2)
        osem = [P.new_dma_sem('os0'), P.new_dma_sem('os1')]
        NB = SEQ // 128

        def store_block(i):
            b = i % 2
            t0 = NMETA + i * 128
            for half in range(2):
                pb, pt_ = self.bank("ld", [0, 1, 2, 3])
                for c4 in range(4):
                    c = half * 4 + c4
                    P.add("pe", lambda e, pb=pb, c=c, c4=c4, t0=t0: e.transpose(pb[:, c4 * 128:(c4 + 1) * 128],
                                                                              self.h[:, c, t0:t0 + 128], self.ident[:]),
                          reads=self.htts(c, c + 1, t0, t0 + 128) + [self.t_const], writes=[pt_])
                self.copy("act" if half == 0 else "dve", os_[b][:, half * 512:(half + 1) * 512], pb[:, :], [pt_], [os_tt[b]])
            self.dma("sp", self.out[s_store, i * 128:(i + 1) * 128, :], os_[b][:, :], [os_tt[b]], [], osem[b])

        def load_dma(i):
            b = i % 2
            self.dma("sp", xs[b][:, :], self.x[s_load, i * 128:(i + 1) * 128, :], [], [xs_tt[b]], xsem[b])

        def load_block(i):
            b = i % 2
            t0 = NMETA + i * 128
            for half in range(2):
                pb, pt_ = self.bank("ld2", [4, 5, 6, 7])
                for c4 in range(4):
                    c = half * 4 + c4
                    P.add("pe", lambda e, pb=pb, c=c, c4=c4, b=b: e.transpose(pb[:, c4 * 128:(c4 + 1) * 128],
                                                                            xs[b][:, c * 128:(c + 1) * 128], self.ident[:]),
                          reads=[xs_tt[b], self.t_const], writes=[pt_])
                self.copy("act" if half == 0 else "dve", self.h[:, half * 4:half * 4 + 4, t0:t0 + 128],
                          pb[:, :].rearrange("p (c t) -> p c t", c=4), [pt_], self.htts(half * 4, half * 4 + 4, t0, t0 + 128))

        if s_load is not None:
            self.dma("sp", xm[0:NMETA, :], self.meta, [], [xm_tt], P.new_dma_sem('xm'))
            pb, pt_ = self.bank("ld2", [4, 5, 6, 7])
            for c in range(DC):
                P.add("pe", lambda e, pb=pb, c=c: e.transpose(pb[:, c * 16:(c + 1) * 16], xm[0:NMETA, c * 128:(c + 1) * 128],
                                                              self.ident[0:NMETA, 0:NMETA]),
                      reads=[xm_tt, self.t_const], writes=[pt_])
            self.copy("dve", self.h[:, :, 0:NMETA], pb[:, 0:DC * 16].rearrange("p (c t) -> p c t", c=DC), [pt_],
                      self.htts(0, DC, 0, NMETA))
            load_dma(0)
            load_dma(1)
        for i in range(NB + 1):
            if s_store is not None and i < NB:
                store_block(i)
            if s_load is not None and i >= 1:
                load_block(i - 1)
                if i + 1 < NB:
                    load_dma(i + 1)

    def rstd_from_sq(self, nch, w, inv_n, extra_reads=()):
        k = self.nrm_i % 2
        self.nrm_i += 1
        pb, pt_ = self.ps[7], self.ps_tt[7]
        for c in range(nch):
            self.mm(pb[:, 0:w], self.onesb[:, :], self.sq[:, c, 0:w], c == 0, c == nch - 1,
                    [self.sq_tt, self.t_const], [pt_])
        self.act(self.rs[k][:, 0:w], pb[:, 0:w], AF.Ln, [pt_], [self.rs_tt[k]], scale=inv_n, bias=EPS)
        self.act(self.rstd[k][:, 0:w], self.rs[k][:, 0:w], AF.Exp, [self.rs_tt[k]], [self.rstd_tt[k]], scale=-0.5)
        return self.rstd[k], self.rstd_tt[k]

    def norm_pre(self, t0, w, gcol, out_fn, out_tts, zero_to=None):
        src_tts = self.htts(0, DC, t0, t0 + w)
        self.sq_rot()
        self.act(self.sq[:, :, 0:w], self.h[:, :, t0:t0 + w], AF.Square, src_tts, [self.sq_tt])
        rstd, rtt = self.rstd_from_sq(DC, w, 1.0 / D)
        for c in range(DC):
            self.stt("dve", out_fn(c), self.h[:, c, t0:t0 + w], self.pcol(gcol + c), rstd[:, 0:w], ALU.mult, ALU.mult,
                     self.htts(c, c + 1, t0, t0 + w) + [rtt, self.t_pt], out_tts)

    def post_norm_add(self, raw_fn, raw_tts, t0, w, gcol):
        rstd, rtt = self.rstd_from_sq(DC, w, 1.0 / D)
        for c in range(DC):
            k = self.tp_i % 2
            self.tp_i += 1
            self.tt_op(ENG["postmul"], self.tpost[k][:, 0:w], raw_fn(c), rstd[:, 0:w], ALU.mult, raw_tts + [rtt], [self.tpost_tt[k]])
            hv = self.h[:, c, t0:t0 + w]
            ht = self.htts(c, c + 1, t0, t0 + w)
            self.stt("dve", hv, self.tpost[k][:, 0:w], self.pcol(gcol + c), hv, ALU.mult, ALU.add,
                     [self.tpost_tt[k], self.t_pt] + ht, ht)

    def ffn(self, s, l):
        P = self.P
        O_A, O_WU0, O_CT, O_WU1, O_HID, O_WD, O_HALO = 0, 16544, 24736, 33024, 41216, 86624, 109152
        NG = FC // 2
        a_half = self.av(O_A, [DC, HW + 2], BF16)
        f_raw = self.av(O_A, [DC, HW], F32)
        wu_off = [O_WU0, O_WU1]
        wu = [self.av(o, [2, DC, 256], BF16) for o in wu_off]
        wu_f = [self.av(o, [2 * DC * 256], BF16) for o in wu_off]
        wu_sem = [P.new_dma_sem('wu%d' % i) for i in range(2)]
        wu_tt = [None, self.alloc(O_WU1, 8192)]
        hid = self.av(O_HID, [FC, HW], BF16)
        hid_tt = self.alloc(O_HID, 45408, n=HT)
        halo_sv = self.av(O_HALO, [DC, 2], BF16)
        halo_tt = self.alloc(O_HALO, 32)
        ct = [[self.av(O_CT + (b * 3 + k) * 1376, [TW], F32) for k in range(3)] for b in range(2)]
        wd = [self.av(O_WD + i * 11264, [FC, 256], BF16) for i in range(2)]
        wd_f = [self.av(O_WD + i * 11264, [FC * 256], BF16) for i in range(2)]
        wd_tt = [self.alloc(O_WD + i * 11264, 11264) for i in range(2)]
        wd_sem = [P.new_dma_sem('wd%d' % i) for i in range(2)]
        bufof = lambda g: (g + 1) % 2

        def dma_up(g):
            b = bufof(g)
            self.dma("pool", wu_f[b][:, :], self.w_up[l, g], [], [wu_tt[b]], wu_sem[b])

        def dma_dn(og):
            self.dma("pool", wd_f[og % 2][:, :], self.w_down[l, og], [], [wd_tt[og % 2]], wd_sem[og % 2])

        dma_up(0)
        for hh in range(2):
            hs = hh * HW
            a_tt = self.alloc(O_A, 16544, n=HT + 1)
            wu_tt[0] = self.alloc(O_WU0, 8192)
            ct_tt = [[self.alloc(O_CT + (b * 3 + k) * 1376, 1376) for k in range(3)] for b in range(2)]
            if hh == 0:
                P.add("dve", lambda e: e.memset(a_half[:, :, 0:2], 0.0), writes=[a_tt[0]])
            else:
                self.copy("dve", a_half[:, :, 0:2], halo_sv[:, :, :], [halo_tt], [a_tt[0]])
            for tl in range(HT):
                self.norm_pre(hs + tl * TW, TW, self.G_FFNPRE + l * 8,
                              lambda c, tl=tl: a_half[:, c, 2 + tl * TW:2 + (tl + 1) * TW], [a_tt[1 + tl]])
            if hh == 0:
                self.copy("dve", halo_sv[:, :, :], a_half[:, :, HW:HW + 2], [a_tt[HT]], [halo_tt])
            cnt = 0
            for g in range(NG):
                b = bufof(g)
                if g + 1 < NG:
                    dma_up(g + 1)
                if g == 6:
                    dma_dn(0)
                if g == 8:
                    dma_dn(1)
                for ii in range(2):
                    i = g * 2 + ii
                    for tl in range(HT):
                        cb = cnt % 2
                        cnt += 1
                        rd = [a_tt[tl], a_tt[1 + tl], wu_tt[b]]
                        pg, pgt = self.bank("up", [0, 1, 2, 3, 4, 5])
                        for k in range(DC):
                            self.mm(pg[:, 0:TW + 2], wu[b][:, 0, k, ii * 128:(ii + 1) * 128], a_half[:, k, tl * TW:tl * TW + TW + 2],
                                    k == 0, k == DC - 1, rd, [pgt])
                        pv, pvt = self.bank("up", [0, 1, 2, 3, 4, 5])
                        for k in range(DC):
                            self.mm(pv[:, 0:TW + 2], wu[b][:, 1, k, ii * 128:(ii + 1) * 128], a_half[:, k, tl * TW:tl * TW + TW + 2],
                                    k == 0, k == DC - 1, rd, [pvt])
                        G, V_, SG = ct[cb]
                        Gt, Vt, SGt = ct_tt[cb]
                        cwc = lambda tap, ch: self.pcol(self.G_CW + (l * 3 + tap) * 44 + ch)
                        chain = ((pg, pgt, G, Gt, i), (pv, pvt, V_, Vt, FC + i))
                        for (pp, ppt, dst, dtt, ch) in chain:
                            self.act(dst[:, :], pp[:, 2:TW + 2], AF.Identity, [ppt, self.t_pt], [dtt],
                                     scale=cwc(2, ch), bias=self.pcol(self.G_CB + l * 44 + ch))
                        for tap, lo in ((1, 1), (0, 0)):
                            for (pp, ppt, dst, dtt, ch) in chain:
                                self.stt("dve", dst[:, :], pp[:, lo:lo + TW], cwc(tap, ch), dst[:, :], ALU.mult, ALU.add,
                                         [ppt, dtt, self.t_pt], [dtt])
                        self.act(SG[:, :], G[:, :], AF.Silu, [Gt], [SGt])
                        self.tt_op(ENG["hid"], hid[:, i, tl * TW:(tl + 1) * TW], SG[:, :], V_[:, :], ALU.mult, [SGt, Vt], [hid_tt[tl]])
            if hh == 0:
                dma_up(0)
            f_tt = self.alloc(O_A, 33024, n=HT)
            def dn_unit(og, oi, tl):
                b = og % 2
                o = og * 2 + oi
                pb, pt_ = self.bank("dn", [0, 1, 2, 3, 4, 5, 6])
                for i in range(FC):
                    self.mm(pb[:, 0:TW], wd[b][:, i, oi * 128:(oi + 1) * 128], hid[:, i, tl * TW:(tl + 1) * TW],
                            i == 0, i == FC - 1, [wd_tt[b], hid_tt[tl]], [pt_])
                self.copy("act", f_raw[:, o, tl * TW:(tl + 1) * TW], pb[:, 0:TW], [pt_], [f_tt[tl]])

            def pn(tl):
                self.sq_rot()
                self.act(self.sq[:, :, 0:TW], f_raw[:, :, tl * TW:(tl + 1) * TW], AF.Square, [f_tt[tl]], [self.sq_tt])
                self.post_norm_add(lambda c, tl=tl: f_raw[:, c, tl * TW:(tl + 1) * TW], [f_tt[tl]], hs + tl * TW, TW,
                                   self.G_FFNPOST + l * 8)

            for og in range(3):
                for oi in range(2):
                    for tl in range(HT):
                        dn_unit(og, oi, tl)
                if og + 2 < 4:
                    dma_dn(og + 2)
            for tl in range(HT):
                for oi in range(2):
                    dn_unit(3, oi, tl)
                if tl >= 1:
                    pn(tl - 1)
            pn(HT - 1)

    def pool_mixer(self, s, l):
        P = self.P
        j = l // 2
        HALO = 15
        WW = TW + HALO
        SET = 3 * 11488 + 5504 + 11008
        O_WP = 2 * SET
        O_AH = O_WP + 4096
        Ab = [self.av(i * SET, [DC, WW], F32) for i in range(2)]
        Bb = [self.av(i * SET + 11488, [DC, WW], F32) for i in range(2)]
        Cb = [self.av(i * SET + 22976, [DC, WW], F32) for i in range(2)]
        MXb = [self.av(i * SET + 34464, [DC, TW], BF16) for i in range(2)]
        Yb = [self.av(i * SET + 39968, [DC, TW], F32) for i in range(2)]
        A_t = [self.alloc(i * SET, 11488) for i in range(2)]
        B_t = [self.alloc(i * SET + 11488, 8616) for i in range(2)]
        C_t = [self.alloc(i * SET + 22976, 8616) for i in range(2)]
        MX_t = [self.alloc(i * SET + 34464, 5504) for i in range(2)]
        Y_t = [self.alloc(i * SET + 39968, 11008) for i in range(2)]
        B_t2 = [self.alloc(i * SET + 11488 + 8616, 2872) for i in range(2)]
        C_t2 = [self.alloc(i * SET + 22976 + 8616, 2872) for i in range(2)]
        WP = self.av(O_WP, [4, 2, 256], BF16)
        WP_tt = self.alloc(O_WP, 4096)
        AH = self.av(O_AH, [DC, HALO], F32)
        AH_tt = self.alloc(O_AH, 480)
        self.dma("pool", self.av(O_WP, [2048], BF16)[:, :], self.pool_w[j], [], [WP_tt], P.new_dma_sem('wp'))

        def s1(tt):
            k = tt % 2
            A, B, C, MX = Ab[k], Bb[k], Cb[k], MXb[k]
            A_tt, B_tt, C_tt, MX_tt = A_t[k], B_t[k], C_t[k], MX_t[k]
            t0 = tt * TW
            if tt == 0:
                P.add("dve", lambda e: e.memset(A[:, :, 0:HALO], 0.0), writes=[A_tt])
            else:
                self.copy("dve", A[:, :, 0:HALO], AH[:, :, :], [AH_tt], [A_tt])
            self.norm_pre(t0, TW, self.G_MIXPRE + l * 8, lambda c: A[:, c, HALO:WW], [A_tt])
            self.copy("dve", AH[:, :, :], A[:, :, WW - HALO:WW], [A_tt], [AH_tt])
            self.tt_op("dve", B[:, 0:6, 1:WW], A[:, 0:6, 1:WW], A[:, 0:6, 0:WW - 1], ALU.add, [A_tt], [B_tt])
            self.tt_op("dve", C[:, 2:6, 2:WW], B[:, 2:6, 2:WW], B[:, 2:6, 0:WW - 2], ALU.add, [B_tt], [C_tt])
            self.tt_op("dve", B[:, 4:6, 7:WW], C[:, 4:6, 7:WW], C[:, 4:6, 3:WW - 4], ALU.add, [C_tt, B_tt], [B_tt])
            self.tt_op(ENG["pooladd"], B[:, 6:DC, 1:WW], A[:, 6:DC, 1:WW], A[:, 6:DC, 0:WW - 1], ALU.add, [A_tt], [B_t2[k]])
            self.tt_op(ENG["pooladd"], C[:, 6:DC, 2:WW], B[:, 6:DC, 2:WW], B[:, 6:DC, 0:WW - 2], ALU.add, [B_t2[k]], [C_t2[k]])
            self.tt_op(ENG["pooladd"], B[:, 6:DC, 7:WW], C[:, 6:DC, 7:WW], C[:, 6:DC, 3:WW - 4], ALU.add, [C_t2[k], B_t2[k]], [B_t2[k]])
            self.tt_op(ENG["pooladd"], C[:, 6:DC, 15:WW], B[:, 6:DC, 15:WW], B[:, 6:DC, 7:WW - 8], ALU.add, [B_t2[k], C_t2[k]], [C_t2[k]])
            for g, (src, stt_) in enumerate(((B, B_tt), (C, C_tt), (B, B_tt), (C, C_t2[k]))):
                w = (2, 4, 8, 16)[g]
                self.stt("dve", MX[:, 2 * g:2 * g + 2, :], src[:, 2 * g:2 * g + 2, HALO:WW], 1.0 / w, A[:, 2 * g:2 * g + 2, HALO:WW],
                         ALU.mult, ALU.subtract, [stt_, A_tt], [MX_tt])
                if tt == 0:
                    for cc in range(2):
                        c = 2 * g + cc
                        kk = self.tp_i % 2
                        self.tp_i += 1
                        self.tt_op("dve", self.tpost[kk][:, 0:16], src[:, c, HALO:HALO + 16], self.invc[:, g, :], ALU.mult,
                                   [stt_, self.t_const], [self.tpost_tt[kk]])
                        self.tt_op("dve", MX[:, c, 0:16], self.tpost[kk][:, 0:16], A[:, c, HALO:HALO + 16], ALU.subtract,
                                   [self.tpost_tt[kk], A_tt, MX_tt], [MX_tt])

        def s2(tt):
            k = tt % 2
            MX, Y, MX_tt, Y_tt = MXb[k], Yb[k], MX_t[k], Y_t[k]
            for oc in range(DC):
                g, oi = divmod(oc, 2)
                pb, pt_ = self.bank("dn", [0, 1, 2, 3, 4, 5, 6])
                for kc in range(2):
                    self.mm(pb[:, 0:TW], WP[:, g, kc, oi * 128:(oi + 1) * 128], MX[:, 2 * g + kc, :], kc == 0, kc == 1,
                            [WP_tt, MX_tt], [pt_])
                sc = self.pcol(self.G_PS + j * 8 + oc)
                self.act(Y[:, oc, :], pb[:, 0:TW], AF.Identity, [pt_, self.t_pt], [Y_tt], scale=sc)
            self.sq_rot()
            self.act(self.sq[:, :, 0:TW], Y[:, :, :], AF.Square, [Y_tt], [self.sq_tt])
            self.post_norm_add(lambda c: Y[:, c, :], [Y_tt], tt * TW, TW, self.G_MIXPOST + l * 8)

        s1(0)
        for tt in range(NT):
            if tt + 1 < NT:
                s1(tt + 1)
            s2(tt)

    def mla(self, s, l):
        P = self.P
        j = l // 2
        O_TAB, O_OT, O_CQ, O_CKV, O_KR, O_PH = 0, 16512, 49536, 61920, 70176, 74304
        Ct = self.av(O_TAB, [LT], F32, 64, 96)
        St = self.av(O_TAB + 8256, [LT], F32, 64, 96)
        tab_tt = self.alloc(O_TAB, 16512)
        self.dma("sp", Ct[:, :], self.rope_cs[0], [], [tab_tt], P.new_dma_sem('tab'))
        self.dma("sp", St[:, :], self.rope_cs[1], [], [tab_tt], P.new_dma_sem('tab'))
        a_t = [self.av(O_OT + i * 5504, [DC, TW], BF16) for i in range(2)]
        a_tt = [self.alloc(O_OT + i * 5504, 5504) for i in range(2)]
        cq = self.av(O_CQ, [3, LT], BF16)
        ckv = self.av(O_CKV, [2, LT], BF16)
        cq_tt = self.alloc(O_CQ, 12384, n=NT)
        ckv_tt = self.alloc(O_CKV, 8256, n=NT)
        krope = self.av(O_KR, [LT], BF16, 64, 96)
        kr_tt = self.alloc(O_KR, 4128)
        O_WD, O_RAW, O_RT = O_PH, O_PH + 11264, O_PH + 11264 + 6880
        Wd = self.av(O_WD, [DC, 704], BF16)
        Wd_tt = self.alloc(O_WD, 11264)
        raw = self.av(O_RAW, [5, TW], F32)
        raw_tt = self.alloc(O_RAW, 6880, n=2)
        rt = [self.av(O_RT + i * 1376, [TW], F32, 64, 96) for i in range(2)]
        rt_tt = [self.alloc(O_RT + i * 1376, 1376) for i in range(2)]
        sem = P.new_dma_sem('wdq')
        self.dma("pool", self.av(O_WD, [DC * 704], BF16)[:, :], self.w_dqkv[j], [], [Wd_tt], sem)
        P.add("dve", lambda e: e.tensor_scalar_mul(out=Wd[:, :, 672:688], in0=Wd[:, :, 672:688], scalar1=-1.0),
              reads=[Wd_tt], writes=[Wd_tt])
        self.norm_pre(0, TW, self.G_MIXPRE + l * 8, lambda c: a_t[0][:, c, :], [a_tt[0]])
        for tt in range(NT):
            t0 = tt * TW
            ab = tt % 2
            for oc in range(5):
                pb, pt_ = self.ps[oc], self.ps_tt[oc]
                for k in range(DC):
                    self.mm(pb[:, 0:TW], Wd[:, k, oc * 128:(oc + 1) * 128], a_t[ab][:, k, :], k == 0, k == DC - 1,
                            [Wd_tt, a_tt[ab]], [pt_])
            for (bk, c0) in ((5, 640), (6, 672)):
                pb, pt_ = self.ps[bk], self.ps_tt[bk]
                for k in range(DC):
                    self.mm(pb[64:96, 0:TW], Wd[:, k, c0:c0 + 32], a_t[ab][:, k, :], k == 0, k == DC - 1,
                            [Wd_tt, a_tt[ab]], [pt_], tp=(0, 64))
            if tt + 1 < NT:
                self.norm_pre(t0 + TW, TW, self.G_MIXPRE + l * 8, lambda c, ab=ab: a_t[1 - ab][:, c, :], [a_tt[1 - ab]])
            for (c0, nch, gcol, dst, dtt, ri) in ((0, 3, self.G_Q + j * 3, cq, cq_tt, 0), (3, 2, self.G_KV + j * 2, ckv, ckv_tt, 1)):
                self.sq_rot()
                for c in range(nch):
                    self.copy("act", raw[:, c0 + c, :], self.ps[c0 + c][:, 0:TW], [self.ps_tt[c0 + c]], [raw_tt[ri]])
                    self.act(self.sq[:, c, 0:TW], self.ps[c0 + c][:, 0:TW], AF.Square, [self.ps_tt[c0 + c]], [self.sq_tt])
                rstd, rtt = self.rstd_from_sq(nch, TW, 1.0 / (nch * 128))
                for c in range(nch):
                    self.stt("dve", dst[:, c, t0:t0 + TW], raw[:, c0 + c, :], self.pcol(gcol + c), rstd[:, 0:TW], ALU.mult, ALU.mult,
                             [raw_tt[ri], rtt, self.t_pt], [dtt[tt]])
            self.tt_op("dve", rt[0][:, :], self.ps[5][64:96, 0:TW], Ct[:, t0:t0 + TW], ALU.mult, [self.ps_tt[5], tab_tt], [rt_tt[0]])
            self.tt_op("dve", rt[1][:, :], self.ps[6][64:96, 0:TW], St[:, t0:t0 + TW], ALU.mult, [self.ps_tt[6], tab_tt], [rt_tt[1]])
            self.tt_op("dve", krope[:, t0:t0 + TW], rt[0][:, :], rt[1][:, :], ALU.add, [rt_tt[0], rt_tt[1]], [kr_tt])
        oT = self.av(O_OT, [DC, LT], BF16)
        oT_tt = self.alloc(O_OT, 33024, n=NT)
        o = O_PH
        kT = [self.av(o + i * 4128, [LT], BF16, 0, 96) for i in range(2)]
        kTn_tt = [self.alloc(o + i * 4128, 4128) for i in range(2)]
        kTr_tt = [TT(), TT()]
        o += 8256
        qT = [self.av(o + i * 688, [TW], BF16, 0, 96) for i in range(3)]
        qTn_tt = [self.alloc(o + i * 688, 688) for i in range(3)]
        qTr_tt = [TT(), TT(), TT()]
        o += 2064
        Va = self.av(o, [NKB, 2, 128], BF16)
        O_VA = o
        Va_tt = self.alloc(o, 8704)
        o += 8704
        Wq = [self.av(o + i * 1152, [3, 192], BF16) for i in range(2)]
        Wq_f = [self.av(o + i * 1152, [576], BF16) for i in range(2)]
        Wq_tt = [self.alloc(o + i * 1152, 1152) for i in range(2)]
        o += 2304
        Wr = [self.av(o + i * 384, [3, 2, 32], BF16) for i in range(2)]
        Wr_tt = [self.alloc(o + i * 384, 384) for i in range(2)]
        o += 768
        Wk = [self.av(o + i * 1024, [2, 256], BF16) for i in range(2)]
        Wk_f = [self.av(o + i * 1024, [512], BF16) for i in range(2)]
        Wk_tt = [self.alloc(o + i * 1024, 1024) for i in range(2)]
        o += 2048
        PTb = [self.av(o + i * 688, [TW], BF16) for i in range(4)]
        PT_tt = [self.alloc(o + i * 688, 688) for i in range(4)]
        o += 2752
        qr = [self.av(o + i * 1376, [TW], F32, 64, 96) for i in range(4)]
        qr_tt = [self.alloc(o + i * 1376, 1376) for i in range(4)]
        o += 5504
        num = [self.av(o + i * 1376, [TW], F32) for i in range(2)]
        num_tt = [self.alloc(o + i * 1376, 1376) for i in range(2)]
        o += 2752
        rden = [self.av(o + i * 1376, [TW], F32) for i in range(2)]
        rden_tt = [self.alloc(o + i * 1376, 1376) for i in range(2)]
        o += 2752
        rdb = [self.av(o + i * 1376, [TW], F32) for i in range(2)]
        rdb_tt = [self.alloc(o + i * 1376, 1376) for i in range(2)]
        rdb_sem = [P.new_dma_sem('rdb%d' % i) for i in range(2)]
        o += 2752
        wsem = [P.new_dma_sem('wh%d' % i) for i in range(2)]
        P.add("pool", lambda e: e.memset(Va[:, :, 0, 64:128], 1.0), writes=[Va_tt])
        P.add("pool", lambda e: e.memset(Va[:, :, 1, 0:64], 1.0), writes=[Va_tt])
        wsem_k = [P.new_dma_sem('whk%d' % i) for i in range(2)]

        def pair_dma(hp):
            wb = hp % 2
            self.dma("pool", Wq_f[wb][:, :], self.w_uq[j, hp], [], [Wq_tt[wb]], wsem[wb])
            self.dma("pool", Wk_f[wb][:, :], self.w_ukv[j, hp], [], [Wk_tt[wb]], wsem_k[wb])

        def wr_build(hp):
            wb = hp % 2
            wq4 = Wq[wb][:, :, :].rearrange("p k (h d) -> p k h d", h=2)
            P.add("dve", lambda e, wb=wb, wq4=wq4: e.tensor_scalar_mul(out=Wr[wb][:, :, :, 0:16], in0=wq4[:, :, :, 80:96], scalar1=-1.0),
                  reads=[Wq_tt[wb]], writes=[Wr_tt[wb]])
            self.copy("dve", Wr[wb][:, :, :, 16:32], wq4[:, :, :, 64:80], [Wq_tt[wb]], [Wr_tt[wb]])

        def v_build(hp):
            wb = hp % 2
            rhs = Wk[wb][:, :, :].rearrange("p k (h d) -> p k h d", h=2)
            Vflat = self.av(O_VA, [NKB * 256], BF16)
            for kg in range(0, NKB, 4):
                nb4 = min(4, NKB - kg)
                rows = min(128, LT - kg * 128)
                pb, pt_ = self.bank("att_s", [0, 1, 2])
                for bi in range(nb4):
                    kb = kg + bi
                    for k in range(2):
                        self.mm(pb[0:rows, bi * 128:(bi + 1) * 128], ckv[:, k, kb * 128:kb * 128 + rows], rhs[:, k, :, 64:128],
                                k == 0, k == 1, ckv_tt + [Wk_tt[wb]], [pt_])
                dst = bass.AP(Vflat.tensor, Vflat[0:rows, kg * 256:kg * 256 + 1].offset,
                              [list(Vflat[0:rows, :].ap[0]), [256, nb4], [192, 2], [1, 64]])
                src = pb[0:rows, 0:nb4 * 128].rearrange("p (b h d) -> p b h d", b=nb4, h=2)
                self.copy("dve", dst, src, [pt_], [Va_tt])

        def gen_k(hd, tt):
            wb, hi, kb_ = (hd // 2) % 2, hd % 2, hd % 2
            t0 = tt * TW
            pb, pt_ = self.ps[7], self.ps_tt[7]
            for k in range(2):
                self.mm(pb[0:64, 0:TW], Wk[wb][:, k, hi * 128:hi * 128 + 64], ckv[:, k, t0:t0 + TW], k == 0, k == 1,
                        [Wk_tt[wb], ckv_tt[tt]], [pt_])
                if k == 0:
                    yield
            self.copy("dve", kT[kb_][0:64, t0:t0 + TW], pb[0:64, 0:TW], [pt_], [kTn_tt[kb_]])
            if tt == NT - 1:
                self.copy(ENG["krope"], kT[kb_][64:96, :], krope[:, :], [kr_tt], [kTr_tt[kb_]])
            yield

        def gen_q(hd, tt, ni):
            wb, hi = (hd // 2) % 2, hd % 2
            t0 = tt * TW
            qb = ni % 3
            pq, pqt = self.ps[5], self.ps_tt[5]
            pr, prt = self.ps[6], self.ps_tt[6]
            for k in range(3):
                self.mm(pq[0:96, 0:TW], Wq[wb][:, k, hi * 96:(hi + 1) * 96], cq[:, k, t0:t0 + TW], k == 0, k == 2,
                        [Wq_tt[wb], cq_tt[tt]], [pqt])
                yield
            for k in range(3):
                self.mm(pr[64:96, 0:TW], Wr[wb][:, k, hi, :], cq[:, k, t0:t0 + TW], k == 0, k == 2,
                        [Wr_tt[wb], cq_tt[tt]], [prt], tp=(0, 64))
                if k < 2:
                    yield
            self.copy("dve", qT[qb][0:64, :], pq[0:64, 0:TW], [pqt], [qTn_tt[qb]])
            rb = ni % 2
            r0, r1 = qr[rb * 2], qr[rb * 2 + 1]
            r0t, r1t = qr_tt[rb * 2], qr_tt[rb * 2 + 1]
            self.tt_op("dve", r0[:, :], pq[64:96, 0:TW], Ct[:, t0:t0 + TW], ALU.mult, [pqt, tab_tt], [r0t])
            self.tt_op("dve", r1[:, :], pr[64:96, 0:TW], St[:, t0:t0 + TW], ALU.mult, [prt, tab_tt], [r1t])
            self.tt_op(ENG["ropeadd"], qT[qb][64:96, :], r0[:, :], r1[:, :], ALU.add, [r0t, r1t], [qTr_tt[qb]])
            yield

        def drain(g):
            for _ in g:
                pass

        def emit_att(hd, tt, ni, filler):
            hp, hi, kb_ = hd // 2, hd % 2, hd % 2
            t0 = tt * TW
            qb = ni % 3
            po, pot = self.bank("att_o", [3, 4])
            q_end = t0 + TW
            nkb = (q_end - 1) // 128 + 1
            info = {}

            def s_emit(kb):
                k0 = kb * 128
                rows = min(128, LT - k0)
                c_lo = max(0, k0 - t0)
                pb, pt_ = self.bank("att_s", [0, 1, 2])
                full = (k0 + rows - 1) <= t0 + c_lo
                self.mm(pb[0:rows, c_lo:TW], kT[kb_][0:96, k0:k0 + rows], qT[qb][0:96, c_lo:TW], True, full,
                        [kTn_tt[kb_], kTr_tt[kb_], qTn_tt[qb], qTr_tt[qb]], [pt_])
                if not full:
                    mc_hi = min(k0 + 128, q_end) - t0
                    m0 = t0 + c_lo - k0
                    m1 = t0 + mc_hi - k0
                    self.mm(pb[0:rows, c_lo:mc_hi], self.identb[0:rows, 0:rows], self.maskb[0:rows, m0:m1], False, True,
                            [self.t_const], [pt_])
                pk = self.rr.get("ptb", 0) % 4
                self.rr["ptb"] = self.rr.get("ptb", 0) + 1
                self.act(PTb[pk][0:rows, c_lo:TW], pb[0:rows, c_lo:TW], AF.Exp, [pt_], [PT_tt[pk]], scale=SCALE)
                info[kb] = (rows, c_lo, pk)

            def pv_emit(kb):
                rows, c_lo, pk = info[kb]
                self.mm(po[:, c_lo:TW], Va[0:rows, kb, hi, :], PTb[pk][0:rows, c_lo:TW], kb == 0, kb == nkb - 1,
                        [Va_tt, PT_tt[pk]], [pot])

            s_emit(0)
            if nkb > 1:
                s_emit(1)
            for kb in range(nkb):
                if kb + 2 < nkb:
                    s_emit(kb + 2)
                pv_emit(kb)
                next(filler, None)
            drain(filler)
            nb = ni % 2
            n0, d0 = (0, 64) if hi == 0 else (64, 0)
            P.add("dve", lambda e, nb=nb, d0=d0, po=po: e.reciprocal(out=rden[nb][d0:d0 + 64, :], in_=po[d0:d0 + 64, 0:TW]),
                  reads=[pot], writes=[rden_tt[nb]])
            self.copy("dve", num[nb][n0:n0 + 64, :], po[n0:n0 + 64, 0:TW], [pot], [num_tt[nb]])
            self.dma("sp", rdb[nb][n0:n0 + 64, :], rden[nb][d0:d0 + 64, :], [rden_tt[nb]], [rdb_tt[nb]], rdb_sem[nb])
            self.tt_op(ENG["otmul"], oT[n0:n0 + 64, hp, t0:t0 + TW], num[nb][n0:n0 + 64, :], rdb[nb][n0:n0 + 64, :], ALU.mult,
                       [num_tt[nb], rdb_tt[nb]], [oT_tt[tt]])

        import itertools
        wo_box = []
        units = [(hd, tt) for hd in range(NH) for tt in range(NT)]
        pair_dma(0)
        wr_build(0)
        v_build(0)
        for t2 in range(NT):
            drain(gen_k(0, t2))
        drain(gen_q(0, 0, 0))
        drain(gen_q(0, 1, 1))
        for ui, (hd, tt) in enumerate(units):
            hp, hi = hd // 2, hd % 2
            if hi == 0 and tt == 0:
                if hp + 1 < NH // 2:
                    pair_dma(hp + 1)
                if hp > 0:
                    v_build(hp)
            if hi == 1 and tt == 0 and hp + 1 < NH // 2:
                wr_build(hp + 1)
            fill = []
            if ui + 2 < len(units):
                nhd, ntt = units[ui + 2]
                fill.append(gen_q(nhd, ntt, ui + 2))
            if hd + 1 < NH:
                fill.append(gen_k(hd + 1, tt))
            emit_att(hd, tt, ui, itertools.chain(*fill))
            if ui == len(units) - 3:
                wo_box.append(self.alloc(O_CQ, 16384))
                self.dma("pool", self.av(O_CQ, [DC * D], BF16)[:, :], self.w_o[j], [], [wo_box[0]], P.new_dma_sem('wo'))
        Wo = self.av(O_CQ, [DC, D], BF16)
        Wo_tt = wo_box[0]
        mrs = [self.av(O_PH + i * 11008, [DC, TW], F32) for i in range(2)]
        mrs_tt = [self.alloc(O_PH + i * 11008, 11008) for i in range(2)]

        def wo_mm(tt):
            t0 = tt * TW
            mr, mr_tt = mrs[tt % 2], mrs_tt[tt % 2]
            for oc in range(DC):
                pb, pt_ = self.bank("dn", [0, 1, 2, 3, 4, 5, 6])
                for kc in range(DC):
                    self.mm(pb[:, 0:TW], Wo[:, kc, oc * 128:(oc + 1) * 128], oT[:, kc, t0:t0 + TW], kc == 0, kc == DC - 1,
                            [Wo_tt, oT_tt[tt]], [pt_])
                self.copy("act", mr[:, oc, :], pb[:, 0:TW], [pt_], [mr_tt])

        def wo_pn(tt):
            mr, mr_tt = mrs[tt % 2], mrs_tt[tt % 2]
            self.sq_rot()
            self.act(self.sq[:, :, 0:TW], mr[:, :, :], AF.Square, [mr_tt], [self.sq_tt])
            self.post_norm_add(lambda c: mr[:, c, :], [mr_tt], tt * TW, TW, self.G_MIXPOST + l * 8)

        wo_mm(0)
        for tt in range(NT):
            if tt + 1 < NT:
                wo_mm(tt + 1)
            wo_pn(tt)

    def build(self):
        self.setup()
        nseq = self.cfg.get("nseq", 2)
        for s in range(nseq):
            if s == 0:
                self.seq_io(None, 0)
            for (l, do_mix, do_ffn) in self.cfg["layers"]:
                if do_mix:
                    if l % 2 == 0:
                        self.mla(s, l)
                    else:
                        self.pool_mixer(s, l)
                if do_ffn:
                    self.ffn(s, l)
            self.seq_io(s, s + 1 if s + 1 < nseq else None)
        self.P.finalize(self.st)


FULL_CFG = {"layers": [(0, True, True), (1, True, True), (2, True, True), (3, True, True)], "nseq": 2}


def build_nc(cfg=None):
    cfg = cfg or FULL_CFG
    nc = bass.Bass("TRN2", target_bir_lowering=False)
    st = ExitStack()
    with st:
        mk = MK(nc, cfg, st)
        mk.build()
    return nc


def rope_consts():
    inv = (1.0 / (np.float32(10000.0) ** (np.arange(0, RR, 2, dtype=np.float32) / np.float32(RR)))).astype(np.float32)
    ang = (np.arange(LT, dtype=np.float32)[None, :] * inv[:, None]).astype(np.float32)
    c, s = np.cos(ang).astype(np.float32), np.sin(ang).astype(np.float32)
    return np.ascontiguousarray(np.stack([np.concatenate([c, c], 0), np.concatenate([s, s], 0)], 0))


def relayout(w):
    c = np.ascontiguousarray
    o = {}
    wd = w["mla_w_dqkv"].reshape(2, DC, 128, 672)
    wd = np.concatenate([wd, wd[..., 656:672], wd[..., 640:656]], axis=-1)
    o["mla_w_dqkv"] = c(wd.transpose(0, 2, 1, 3)).reshape(2, 128, DC * 704)
    o["mla_w_uq"] = c(w["mla_w_uq"].reshape(2, 3, 128, 8, 192).transpose(0, 3, 2, 1, 4)).reshape(2, 8, 128, 576)
    o["mla_w_ukv"] = c(w["mla_w_ukv"].reshape(2, 2, 128, 8, 256).transpose(0, 3, 2, 1, 4)).reshape(2, 8, 128, 512)
    o["mla_w_o"] = c(w["mla_w_o"].reshape(2, DC, 128, D).transpose(0, 2, 1, 3)).reshape(2, 128, DC * D)
    o["pool_w_group"] = c(w["pool_w_group"].reshape(2, 4, 2, 128, 256).transpose(0, 3, 1, 2, 4)).reshape(2, 128, 2048)
    o["ffn_w_up"] = c(w["ffn_w_up"].reshape(4, DC, 128, 2, FC // 2, 256).transpose(0, 4, 2, 3, 1, 5)).reshape(4, FC // 2, 128, 4096)
    o["ffn_w_down"] = c(w["ffn_w_down"].reshape(4, FC, 128, 4, 256).transpose(0, 3, 2, 1, 4)).reshape(4, 4, 128, FC * 256)
    return o


_NC_CACHE = {}


def run(inputs, cfg=None, trace=False, ncores=NCORES):
    key = repr(cfg)
    if key not in _NC_CACHE:
        _NC_CACHE[key] = build_nc(cfg)
    nc = _NC_CACHE[key]
    x = np.ascontiguousarray(np.asarray(inputs["x"], dtype=np.float32))
    shared = {k: np.ascontiguousarray(np.asarray(v, dtype=np.float32)) for k, v in inputs.items() if k != "x"}
    shared.update(relayout(shared))
    shared["rope_cs"] = rope_consts()
    in_maps = []
    for c in range(ncores):
        m = dict(shared)
        m["x"] = np.ascontiguousarray(x[2 * c:2 * c + 2])
        in_maps.append(m)
    res = run_bass_kernel_spmd(nc, in_maps, core_ids=list(range(ncores)), **({"trace": True} if trace else {}))
    out = np.concatenate([r["out"] for r in res.results], axis=0)
    return out.astype(np.float32), res


def kernel(**inputs):
    out, _ = run(inputs)
    return out
```

```python
import numpy as np
from contextlib import ExitStack
import concourse.bass as bass
import concourse.mybir as mybir
from concourse.bass_utils import run_bass_kernel_spmd

F32 = mybir.dt.float32
BF16 = mybir.dt.bfloat16
U8 = mybir.dt.uint8
ALU = mybir.AluOpType
AF = mybir.ActivationFunctionType

NCORES = 8
D = 1024
DC = 8
NMETA = 16
SEQ = 2048
LT = NMETA + SEQ
TW = 344
NT = 6
HT = 3
HW = HT * TW
DFF = 2816
FC = 22
NH = 16
QR, KVR, RR = 384, 256, 32
EPS = 1e-6
NKB = 17
ARENA = 135808
SCALE = 96 ** -0.5
NEG = -30000.0

ENGS = ("pe", "act", "dve", "pool", "sp")
import os as _os
STRICT = bool(int(_os.environ.get("MK_STRICT", "0")))
_E = dict(kv.split("=") for kv in _os.environ.get("MK_ENG", "").split(",") if kv)
ENG = {"postmul": "dve", "hid": "dve", "pooladd": "dve", "krope": "act", "ropeadd": "pool", "otmul": "pool"}
ENG.update(_E)


class TT:
    __slots__ = ("w", "r")

    def __init__(self):
        self.w = None
        self.r = []


class Op:
    __slots__ = ("eng", "emit", "deps", "inc", "sem", "val")

    def __init__(self, eng, emit, deps, inc, sem):
        self.eng, self.emit, self.deps, self.inc, self.sem, self.val = eng, emit, deps, inc, sem, None


class Prog:
    def __init__(self, nc):
        self.nc = nc
        self.ops = {e: [] for e in ENGS}
        self.n_dma_sem = 0

    def new_dma_sem(self, name=None):
        if name is not None:
            if not hasattr(self, "_named"):
                self._named = {}
            if name not in self._named:
                self.n_dma_sem += 1
                self._named[name] = ("dma", self.n_dma_sem)
            return self._named[name]
        self.n_dma_sem += 1
        return ("dma", self.n_dma_sem)

    def add(self, eng, emit, reads=(), writes=(), dma_sem=None, accum=False):
        deps = []
        is_dma = dma_sem is not None

        def need(d, raw):
            if d is None:
                return
            if (not STRICT) and (not raw) and (not is_dma) and d.sem is None and d.eng == eng:
                return
            deps.append(d)
        for t in reads:
            need(t.w, True)
        if not accum:
            for t in writes:
                need(t.w, False)
                for d in t.r:
                    need(d, False)
        op = Op(eng, emit, deps, True, dma_sem)
        self.ops[eng].append(op)
        for t in reads:
            t.r.append(op)
        for t in writes:
            t.w = op
            t.r = []
        return op

    def finalize(self, stack):
        nc = self.nc
        eng_sem = {e: stack.enter_context(nc.semaphore("c_" + e)) for e in ENGS}
        dsem, dcount = {}, {}
        for e in ENGS:
            c = 0
            for op in self.ops[e]:
                if op.sem is not None:
                    if op.sem not in dsem:
                        dsem[op.sem] = stack.enter_context(nc.semaphore("d%d" % op.sem[1]))
                        dcount[op.sem] = 0
                    dcount[op.sem] += 16
                    op.val = dcount[op.sem]
                else:
                    c += 1
                    op.val = c
        engobj = {"pe": "tensor", "act": "scalar", "dve": "vector", "pool": "gpsimd", "sp": "sync"}
        with nc.Block() as block:
            def run(e, eng):
                seen = {}
                for op in self.ops[e]:
                    need = {}
                    for d in op.deps:
                        s = dsem[d.sem] if d.sem is not None else eng_sem[d.eng]
                        key = id(s)
                        if seen.get(key, 0) >= d.val:
                            continue
                        if key not in need or need[key][1] < d.val:
                            need[key] = (s, d.val)
                    for key, (s, v) in need.items():
                        eng.wait_ge(s, v)
                        seen[key] = v
                    ins = op.emit(eng)
                    if op.sem is not None:
                        ins.then_inc(dsem[op.sem], 16)
                    else:
                        ins.then_inc(eng_sem[e], 1)
                fin = {}
                for op in self.ops[e]:
                    if op.sem is not None:
                        fin[op.sem] = max(fin.get(op.sem, 0), op.val)
                for k, v in fin.items():
                    if seen.get(id(dsem[k]), 0) < v:
                        eng.wait_ge(dsem[k], v)

            for e in ENGS:
                getattr(block, engobj[e])(lambda eng, e=e: run(e, eng))


class MK:
    def __init__(self, nc, cfg, st):
        self.nc, self.cfg, self.st = nc, cfg, st
        self.P = Prog(nc)
        P = self.P
        dt = nc.dram_tensor
        self.x = dt("x", [2, SEQ, D], F32, kind="ExternalInput").ap()
        self.meta = dt("meta_tokens", [NMETA, D], F32, kind="ExternalInput").ap()
        self.norms = [dt(n, [4, D], F32, kind="ExternalInput").ap()
                      for n in ("norm_mix_pre", "norm_mix_post", "norm_ffn_pre", "norm_ffn_post")]
        self.w_dqkv = dt("mla_w_dqkv", [2, 128, DC * 704], F32, kind="ExternalInput").ap()
        self.q_norm = dt("mla_q_norm", [2, QR], F32, kind="ExternalInput").ap()
        self.w_uq = dt("mla_w_uq", [2, 8, 128, 3 * 192], F32, kind="ExternalInput").ap()
        self.kv_norm = dt("mla_kv_norm", [2, KVR], F32, kind="ExternalInput").ap()
        self.w_ukv = dt("mla_w_ukv", [2, 8, 128, 2 * 256], F32, kind="ExternalInput").ap()
        self.w_o = dt("mla_w_o", [2, 128, DC * D], F32, kind="ExternalInput").ap()
        self.pool_w = dt("pool_w_group", [2, 128, 4 * 2 * 256], F32, kind="ExternalInput").ap()
        self.pool_scale = dt("pool_scale", [2, D], F32, kind="ExternalInput").ap()
        self.w_up = dt("ffn_w_up", [4, FC // 2, 128, 2 * DC * 256], F32, kind="ExternalInput").ap()
        self.conv_w = dt("ffn_conv_w", [4, 3, 2 * DFF], F32, kind="ExternalInput").ap()
        self.conv_b = dt("ffn_conv_b", [4, 2 * DFF], F32, kind="ExternalInput").ap()
        self.w_down = dt("ffn_w_down", [4, 4, 128, FC * 256], F32, kind="ExternalInput").ap()
        self.rope_cs = dt("rope_cs", [2, RR, LT], F32, kind="ExternalInput").ap()
        self.out = dt("out", [2, SEQ, D], F32, kind="ExternalOutput").ap()

        sb = lambda name, shape, d: st.enter_context(nc.sbuf_tensor(name, shape, d))
        self.h = sb("h", [128, DC, LT], F32)
        self.arena = sb("arena", [128, ARENA], U8)
        self.PT = sb("ptab", [128, 896], F32)
        self.ident = sb("ident", [128, 128], F32)
        self.identb = sb("identb", [128, 128], BF16)
        self.onesb = sb("onesb", [128, 128], BF16)
        self.maskb = sb("maskb", [128, 128], BF16)
        self.invc = sb("invc", [128, 4, 16], F32)
        self.ps = [st.enter_context(nc.psum_tensor("ps%d" % i, [128, 512], F32)) for i in range(8)]
        self.ps_tt = [TT() for _ in range(8)]
        self.h_tt = [[TT() for _ in range(NT)] for _ in range(DC)]
        self.t_const = TT()
        self.t_pt = TT()
        self.live = []
        o = ARENA - 20160
        self.sqs = [self.av(o + i * 5760, [DC, 360], BF16) for i in range(2)]
        self.sq_tts = [TT(), TT()]
        self.sq_i = 0
        self.sq, self.sq_tt = self.sqs[0], self.sq_tts[0]
        o += 11520
        self.rs = [self.av(o + i * 1440, [360], F32) for i in range(2)]
        self.rstd = [self.av(o + 2880 + i * 1440, [360], F32) for i in range(2)]
        self.rs_tt = [TT(), TT()]
        self.rstd_tt = [TT(), TT()]
        o += 5760
        self.tpost = [self.av(o + i * 1440, [360], F32) for i in range(2)]
        self.tpost_tt = [TT(), TT()]
        self.PHASE_END = ARENA - 20160
        self.nrm_i = 0
        self.tp_i = 0
        self.rr = {}

    def av(self, off, shape, dtype, p0=0, p1=128):
        esz = 2 if dtype == BF16 else (4 if dtype == F32 else 1)
        n = int(np.prod(shape))
        assert off % 4 == 0 and off + n * esz <= ARENA, (off, n, esz)
        v = self.arena[p0:p1, off:off + n * esz].bitcast(dtype)
        if len(shape) == 2:
            v = v.rearrange("p (a b) -> p a b", a=shape[0])
        elif len(shape) == 3:
            v = v.rearrange("p (a b c) -> p a b c", a=shape[0], b=shape[1])
        return v

    def alloc(self, off, nbytes, n=1):
        assert off + nbytes <= self.PHASE_END, (off, nbytes)
        tts = [TT() for _ in range(n)]
        keep = []
        for (o0, o1, olds) in self.live:
            if o0 < off + nbytes and off < o1:
                for old in olds:
                    for t in tts:
                        if old.w is not None:
                            t.r.append(old.w)
                        t.r.extend(old.r)
                if not (off <= o0 and o1 <= off + nbytes):
                    keep.append((o0, o1, olds))
            else:
                keep.append((o0, o1, olds))
        keep.append((off, off + nbytes, tts))
        self.live = keep
        return tts if n > 1 else tts[0]

    def sq_rot(self):
        self.sq_i += 1
        self.sq, self.sq_tt = self.sqs[self.sq_i % 2], self.sq_tts[self.sq_i % 2]

    def bank(self, group, banks):
        i = self.rr.get(group, 0)
        self.rr[group] = i + 1
        b = banks[i % len(banks)]
        return self.ps[b], self.ps_tt[b]

    def htts(self, c0, c1, t0, t1):
        return [self.h_tt[c][t] for c in range(c0, c1) for t in range(t0 // TW, (t1 - 1) // TW + 1)]

    def mm(self, out, lhsT, rhs, start, stop, reads, writes, tp=None):
        kw = dict(start=start, stop=stop)
        if tp is not None:
            kw["tile_position"] = tp
        self.P.add("pe", lambda e: e.matmul(out, lhsT=lhsT, rhs=rhs, **kw), reads=reads, writes=writes,
                   accum=not start)

    def act(self, out, in_, func, reads, writes, **kw):
        self.P.add("act", lambda e: e.activation(out=out, in_=in_, func=func, **kw), reads=reads, writes=writes)

    def tt_op(self, eng, out, in0, in1, op, reads, writes):
        self.P.add(eng, lambda e: e.tensor_tensor(out=out, in0=in0, in1=in1, op=op), reads=reads, writes=writes)

    def stt(self, eng, out, in0, scalar, in1, op0, op1, reads, writes):
        self.P.add(eng, lambda e: e.scalar_tensor_tensor(out=out, in0=in0, scalar=scalar, in1=in1, op0=op0, op1=op1),
                   reads=reads, writes=writes)

    def copy(self, eng, out, in_, reads, writes):
        if eng == "act":
            self.P.add("act", lambda e: e.copy(out=out, in_=in_), reads=reads, writes=writes)
        else:
            self.P.add(eng, lambda e: e.tensor_copy(out=out, in_=in_), reads=reads, writes=writes)

    def dma(self, eng, out, in_, reads, writes, sem):
        self.P.add(eng, lambda e: e.dma_start(out=out, in_=in_), reads=reads, writes=writes, dma_sem=sem)

    def pcol(self, col):
        return self.PT[:, col:col + 1]

    def setup(self):
        P = self.P
        tc = self.t_const
        ident, identb, onesb, maskb, invc = self.ident, self.identb, self.onesb, self.maskb, self.invc
        P.add("pool", lambda e: e.memset(ident[:], 1.0), writes=[tc])
        P.add("pool", lambda e: e.affine_select(out=ident[:], in_=ident[:], pattern=[[-1, 128]],
                                                 compare_op=ALU.is_equal, fill=0.0, base=0, channel_multiplier=1),
              reads=[tc], writes=[tc])
        P.add("pool", lambda e: e.tensor_copy(out=identb[:], in_=ident[:]), reads=[tc], writes=[tc])
        P.add("pool", lambda e: e.memset(onesb[:], 1.0), reads=[tc], writes=[tc])
        P.add("pool", lambda e: e.memset(maskb[:], 0.0), reads=[tc], writes=[tc])
        P.add("pool", lambda e: e.affine_select(out=maskb[:], in_=maskb[:], pattern=[[1, 128]],
                                                 compare_op=ALU.is_ge, fill=NEG, base=0, channel_multiplier=-1),
              reads=[tc], writes=[tc])
        for t in range(16):
            for g, w in enumerate((2, 4, 8, 16)):
                P.add("pool", lambda e, g=g, t=t, w=w: e.memset(invc[:, g, t:t + 1], 1.0 / min(t + 1, w)),
                      reads=[tc], writes=[tc])
        segs = []
        col = 0
        for n in self.norms:
            segs.append((n.rearrange("l (c p) -> (l c) p", p=128), 32))
        segs.append((self.q_norm.rearrange("j (c p) -> (j c) p", p=128), 6))
        segs.append((self.kv_norm.rearrange("j (c p) -> (j c) p", p=128), 4))
        segs.append((self.pool_scale.rearrange("j (c p) -> (j c) p", p=128), 16))
        segs.append((self.conv_w.rearrange("l t (c p) -> (l t c) p", p=128), 528))
        segs.append((self.conv_b.rearrange("l (c p) -> (l c) p", p=128), 176))
        if self.cfg.get("skip_params"):
            self.G_MIXPRE, self.G_MIXPOST, self.G_FFNPRE, self.G_FFNPOST = 0, 32, 64, 96
            self.G_Q, self.G_KV, self.G_PS, self.G_CW, self.G_CB = 128, 134, 138, 154, 682
            return
        stg = [self.av(i * 512, [128], F32) for i in range(7)]
        stg_tt = self.alloc(0, 7 * 512, n=7)
        r = 0
        for ap, n in segs:
            a = 0
            while a < n:
                ti, ro = divmod(r, 128)
                m = min(n - a, 128 - ro)
                self.dma("sp", stg[ti][ro:ro + m, :], ap[a:a + m, :], [], [stg_tt[ti]], P.new_dma_sem())
                a += m
                r += m
        self.NPAR = r
        assert r == 858
        for ti in range(7):
            rows = min(128, r - ti * 128)
            pb, pt_ = self.bank("setup", [0, 1])
            self.P.add("pe", lambda e, pb=pb, ti=ti, rows=rows: e.transpose(pb[:, 0:rows], stg[ti][0:rows, :], ident[0:rows, 0:rows]),
                       reads=[stg_tt[ti], tc], writes=[pt_])
            self.copy("dve", self.PT[:, ti * 128:ti * 128 + rows], pb[:, 0:rows], [pt_], [self.t_pt])
        self.G_MIXPRE, self.G_MIXPOST, self.G_FFNPRE, self.G_FFNPOST = 0, 32, 64, 96
        self.G_Q, self.G_KV, self.G_PS, self.G_CW, self.G_CB = 128, 134, 138, 154, 682

    def seq_io(self, s_store, s_load):
        P = self.P
        xs = [self.av(i * 4096, [D], F32) for i in range(2)]
        xs_tt = self.alloc(0, 8192, n=2)
        xsem = [P.new_dma_sem('xs0'), P.new_dma_sem('xs1')]
        xm = self.av(8192, [D], F32)
        xm_tt = self.alloc(8192, 4096)
        os_ = [self.av(12288 + i * 4096, [D], F32) for i in range(2)]
        os_tt = self.alloc(12288, 8192, n=2)
        osem = [P.new_dma_sem('os0'), P.new_dma_sem('os1')]
        NB = SEQ // 128

        def store_block(i):
            b = i % 2
            t0 = NMETA + i * 128
            for half in range(2):
                pb, pt_ = self.bank("ld", [0, 1, 2, 3])
                for c4 in range(4):
                    c = half * 4 + c4
                    P.add("pe", lambda e, pb=pb, c=c, c4=c4, t0=t0: e.transpose(pb[:, c4 * 128:(c4 + 1) * 128],
                                                                              self.h[:, c, t0:t0 + 128], self.ident[:]),
                          reads=self.htts(c, c + 1, t0, t0 + 128) + [self.t_const], writes=[pt_])
                self.copy("act" if half == 0 else "dve", os_[b][:, half * 512:(half + 1) * 512], pb[:, :], [pt_], [os_tt[b]])
            self.dma("sp", self.out[s_store, i * 128:(i + 1) * 128, :], os_[b][:, :], [os_tt[b]], [], osem[b])

        def load_dma(i):
            b = i % 2
            self.dma("sp", xs[b][:, :], self.x[s_load, i * 128:(i + 1) * 128, :], [], [xs_tt[b]], xsem[b])

        def load_block(i):
            b = i % 2
            t0 = NMETA + i * 128
            for half in range(2):
                pb, pt_ = self.bank("ld2", [4, 5, 6, 7])
                for c4 in range(4):
                    c = half * 4 + c4
                    P.add("pe", lambda e, pb=pb, c=c, c4=c4, b=b: e.transpose(pb[:, c4 * 128:(c4 + 1) * 128],
                                                                            xs[b][:, c * 128:(c + 1) * 128], self.ident[:]),
                          reads=[xs_tt[b], self.t_const], writes=[pt_])
                self.copy("act" if half == 0 else "dve", self.h[:, half * 4:half * 4 + 4, t0:t0 + 128],
                          pb[:, :].rearrange("p (c t) -> p c t", c=4), [pt_], self.htts(half * 4, half * 4 + 4, t0, t0 + 128))

        if s_load is not None:
            self.dma("sp", xm[0:NMETA, :], self.meta, [], [xm_tt], P.new_dma_sem('xm'))
            pb, pt_ = self.bank("ld2", [4, 5, 6, 7])
            for c in range(DC):
                P.add("pe", lambda e, pb=pb, c=c: e.transpose(pb[:, c * 16:(c + 1) * 16], xm[0:NMETA, c * 128:(c + 1) * 128],
                                                              self.ident[0:NMETA, 0:NMETA]),
                      reads=[xm_tt, self.t_const], writes=[pt_])
            self.copy("dve", self.h[:, :, 0:NMETA], pb[:, 0:DC * 16].rearrange("p (c t) -> p c t", c=DC), [pt_],
                      self.htts(0, DC, 0, NMETA))
            load_dma(0)
            load_dma(1)
        for i in range(NB + 1):
            if s_store is not None and i < NB:
                store_block(i)
            if s_load is not None and i >= 1:
                load_block(i - 1)
                if i + 1 < NB:
                    load_dma(i + 1)

    def rstd_from_sq(self, nch, w, inv_n, extra_reads=()):
        k = self.nrm_i % 2
        self.nrm_i += 1
        pb, pt_ = self.ps[7], self.ps_tt[7]
        for c in range(nch):
            self.mm(pb[:, 0:w], self.onesb[:, :], self.sq[:, c, 0:w], c == 0, c == nch - 1,
                    [self.sq_tt, self.t_const], [pt_])
        self.act(self.rs[k][:, 0:w], pb[:, 0:w], AF.Ln, [pt_], [self.rs_tt[k]], scale=inv_n, bias=EPS)
        self.act(self.rstd[k][:, 0:w], self.rs[k][:, 0:w], AF.Exp, [self.rs_tt[k]], [self.rstd_tt[k]], scale=-0.5)
        return self.rstd[k], self.rstd_tt[k]

    def norm_pre(self, t0, w, gcol, out_fn, out_tts, zero_to=None):
        src_tts = self.htts(0, DC, t0, t0 + w)
        self.sq_rot()
        self.act(self.sq[:, :, 0:w], self.h[:, :, t0:t0 + w], AF.Square, src_tts, [self.sq_tt])
        rstd, rtt = self.rstd_from_sq(DC, w, 1.0 / D)
        for c in range(DC):
            self.stt("dve", out_fn(c), self.h[:, c, t0:t0 + w], self.pcol(gcol + c), rstd[:, 0:w], ALU.mult, ALU.mult,
                     self.htts(c, c + 1, t0, t0 + w) + [rtt, self.t_pt], out_tts)

    def post_norm_add(self, raw_fn, raw_tts, t0, w, gcol):
        rstd, rtt = self.rstd_from_sq(DC, w, 1.0 / D)
        for c in range(DC):
            k = self.tp_i % 2
            self.tp_i += 1
            self.tt_op(ENG["postmul"], self.tpost[k][:, 0:w], raw_fn(c), rstd[:, 0:w], ALU.mult, raw_tts + [rtt], [self.tpost_tt[k]])
            hv = self.h[:, c, t0:t0 + w]
            ht = self.htts(c, c + 1, t0, t0 + w)
            self.stt("dve", hv, self.tpost[k][:, 0:w], self.pcol(gcol + c), hv, ALU.mult, ALU.add,
                     [self.tpost_tt[k], self.t_pt] + ht, ht)

    def ffn(self, s, l):
        P = self.P
        O_A, O_WU0, O_CT, O_WU1, O_HID, O_WD, O_HALO = 0, 16544, 24736, 33024, 41216, 86624, 109152
        NG = FC // 2
        a_half = self.av(O_A, [DC, HW + 2], BF16)
        f_raw = self.av(O_A, [DC, HW], F32)
        wu_off = [O_WU0, O_WU1]
        wu = [self.av(o, [2, DC, 256], BF16) for o in wu_off]
        wu_f = [self.av(o, [2 * DC * 256], BF16) for o in wu_off]
        wu_sem = [P.new_dma_sem('wu%d' % i) for i in range(2)]
        wu_tt = [None, self.alloc(O_WU1, 8192)]
        hid = self.av(O_HID, [FC, HW], BF16)
        hid_tt = self.alloc(O_HID, 45408, n=HT)
        halo_sv = self.av(O_HALO, [DC, 2], BF16)
        halo_tt = self.alloc(O_HALO, 32)
        ct = [[self.av(O_CT + (b * 3 + k) * 1376, [TW], F32) for k in range(3)] for b in range(2)]
        wd = [self.av(O_WD + i * 11264, [FC, 256], BF16) for i in range(2)]
        wd_f = [self.av(O_WD + i * 11264, [FC * 256], BF16) for i in range(2)]
        wd_tt = [self.alloc(O_WD + i * 11264, 11264) for i in range(2)]
        wd_sem = [P.new_dma_sem('wd%d' % i) for i in range(2)]
        bufof = lambda g: (g + 1) % 2

        def dma_up(g):
            b = bufof(g)
            self.dma("pool", wu_f[b][:, :], self.w_up[l, g], [], [wu_tt[b]], wu_sem[b])

        def dma_dn(og):
            self.dma("pool", wd_f[og % 2][:, :], self.w_down[l, og], [], [wd_tt[og % 2]], wd_sem[og % 2])

        dma_up(0)
        for hh in range(2):
            hs = hh * HW
            a_tt = self.alloc(O_A, 16544, n=HT + 1)
            wu_tt[0] = self.alloc(O_WU0, 8192)
            ct_tt = [[self.alloc(O_CT + (b * 3 + k) * 1376, 1376) for k in range(3)] for b in range(2)]
            if hh == 0:
                P.add("dve", lambda e: e.memset(a_half[:, :, 0:2], 0.0), writes=[a_tt[0]])
            else:
                self.copy("dve", a_half[:, :, 0:2], halo_sv[:, :, :], [halo_tt], [a_tt[0]])
            for tl in range(HT):
                self.norm_pre(hs + tl * TW, TW, self.G_FFNPRE + l * 8,
                              lambda c, tl=tl: a_half[:, c, 2 + tl * TW:2 + (tl + 1) * TW], [a_tt[1 + tl]])
            if hh == 0:
                self.copy("dve", halo_sv[:, :, :], a_half[:, :, HW:HW + 2], [a_tt[HT]], [halo_tt])
            cnt = 0
            for g in range(NG):
                b = bufof(g)
                if g + 1 < NG:
                    dma_up(g + 1)
                if g == 6:
                    dma_dn(0)
                if g == 8:
                    dma_dn(1)
                for ii in range(2):
                    i = g * 2 + ii
                    for tl in range(HT):
                        cb = cnt % 2
                        cnt += 1
                        rd = [a_tt[tl], a_tt[1 + tl], wu_tt[b]]
                        pg, pgt = self.bank("up", [0, 1, 2, 3, 4, 5])
                        for k in range(DC):
                            self.mm(pg[:, 0:TW + 2], wu[b][:, 0, k, ii * 128:(ii + 1) * 128], a_half[:, k, tl * TW:tl * TW + TW + 2],
                                    k == 0, k == DC - 1, rd, [pgt])
                        pv, pvt = self.bank("up", [0, 1, 2, 3, 4, 5])
                        for k in range(DC):
                            self.mm(pv[:, 0:TW + 2], wu[b][:, 1, k, ii * 128:(ii + 1) * 128], a_half[:, k, tl * TW:tl * TW + TW + 2],
                                    k == 0, k == DC - 1, rd, [pvt])
                        G, V_, SG = ct[cb]
                        Gt, Vt, SGt = ct_tt[cb]
                        cwc = lambda tap, ch: self.pcol(self.G_CW + (l * 3 + tap) * 44 + ch)
                        chain = ((pg, pgt, G, Gt, i), (pv, pvt, V_, Vt, FC + i))
                        for (pp, ppt, dst, dtt, ch) in chain:
                            self.act(dst[:, :], pp[:, 2:TW + 2], AF.Identity, [ppt, self.t_pt], [dtt],
                                     scale=cwc(2, ch), bias=self.pcol(self.G_CB + l * 44 + ch))
                        for tap, lo in ((1, 1), (0, 0)):
                            for (pp, ppt, dst, dtt, ch) in chain:
                                self.stt("dve", dst[:, :], pp[:, lo:lo + TW], cwc(tap, ch), dst[:, :], ALU.mult, ALU.add,
                                         [ppt, dtt, self.t_pt], [dtt])
                        self.act(SG[:, :], G[:, :], AF.Silu, [Gt], [SGt])
                        self.tt_op(ENG["hid"], hid[:, i, tl * TW:(tl + 1) * TW], SG[:, :], V_[:, :], ALU.mult, [SGt, Vt], [hid_tt[tl]])
            if hh == 0:
                dma_up(0)
            f_tt = self.alloc(O_A, 33024, n=HT)
            def dn_unit(og, oi, tl):
                b = og % 2
                o = og * 2 + oi
                pb, pt_ = self.bank("dn", [0, 1, 2, 3, 4, 5, 6])
                for i in range(FC):
                    self.mm(pb[:, 0:TW], wd[b][:, i, oi * 128:(oi + 1) * 128], hid[:, i, tl * TW:(tl + 1) * TW],
                            i == 0, i == FC - 1, [wd_tt[b], hid_tt[tl]], [pt_])
                self.copy("act", f_raw[:, o, tl * TW:(tl + 1) * TW], pb[:, 0:TW], [pt_], [f_tt[tl]])

            fsq = {}

            def pn_sq(tl):
                self.sq_rot()
                self.act(self.sq[:, :, 0:TW], f_raw[:, :, tl * TW:(tl + 1) * TW], AF.Square, [f_tt[tl]], [self.sq_tt])
                fsq[tl] = (self.sq, self.sq_tt)

            def pn(tl):
                cur = (self.sq, self.sq_tt)
                self.sq, self.sq_tt = fsq[tl]
                self.post_norm_add(lambda c, tl=tl: f_raw[:, c, tl * TW:(tl + 1) * TW], [f_tt[tl]], hs + tl * TW, TW,
                                   self.G_FFNPOST + l * 8)
                self.sq, self.sq_tt = cur

            for og in range(3):
                for oi in range(2):
                    for tl in range(HT):
                        dn_unit(og, oi, tl)
                if og + 2 < 4:
                    dma_dn(og + 2)
            for tl in range(HT):
                for oi in range(2):
                    dn_unit(3, oi, tl)
                pn_sq(tl)
                if tl >= 1:
                    pn(tl - 1)
            pn(HT - 1)

    def pool_mixer(self, s, l):
        P = self.P
        j = l // 2
        HALO = 15
        WW = TW + HALO
        SET = 3 * 11488 + 5504 + 11008
        O_WP = 2 * SET
        O_AH = O_WP + 4096
        Ab = [self.av(i * SET, [DC, WW], F32) for i in range(2)]
        Bb = [self.av(i * SET + 11488, [DC, WW], F32) for i in range(2)]
        Cb = [self.av(i * SET + 22976, [DC, WW], F32) for i in range(2)]
        MXb = [self.av(i * SET + 34464, [DC, TW], BF16) for i in range(2)]
        Yb = [self.av(i * SET + 39968, [DC, TW], F32) for i in range(2)]
        A_t = [self.alloc(i * SET, 11488) for i in range(2)]
        B_t = [self.alloc(i * SET + 11488, 8616) for i in range(2)]
        C_t = [self.alloc(i * SET + 22976, 8616) for i in range(2)]
        MX_t = [self.alloc(i * SET + 34464, 5504) for i in range(2)]
        Y_t = [self.alloc(i * SET + 39968, 11008) for i in range(2)]
        B_t2 = [self.alloc(i * SET + 11488 + 8616, 2872) for i in range(2)]
        C_t2 = [self.alloc(i * SET + 22976 + 8616, 2872) for i in range(2)]
        WP = self.av(O_WP, [4, 2, 256], BF16)
        WP_tt = self.alloc(O_WP, 4096)
        AH = self.av(O_AH, [DC, HALO], F32)
        AH_tt = self.alloc(O_AH, 480)
        self.dma("pool", self.av(O_WP, [2048], BF16)[:, :], self.pool_w[j], [], [WP_tt], P.new_dma_sem('wp'))

        def s1(tt):
            k = tt % 2
            A, B, C, MX = Ab[k], Bb[k], Cb[k], MXb[k]
            A_tt, B_tt, C_tt, MX_tt = A_t[k], B_t[k], C_t[k], MX_t[k]
            t0 = tt * TW
            if tt == 0:
                P.add("dve", lambda e: e.memset(A[:, :, 0:HALO], 0.0), writes=[A_tt])
            else:
                self.copy("dve", A[:, :, 0:HALO], AH[:, :, :], [AH_tt], [A_tt])
            self.norm_pre(t0, TW, self.G_MIXPRE + l * 8, lambda c: A[:, c, HALO:WW], [A_tt])
            self.copy("dve", AH[:, :, :], A[:, :, WW - HALO:WW], [A_tt], [AH_tt])
            self.tt_op("dve", B[:, 0:6, 1:WW], A[:, 0:6, 1:WW], A[:, 0:6, 0:WW - 1], ALU.add, [A_tt], [B_tt])
            self.tt_op("dve", C[:, 2:6, 2:WW], B[:, 2:6, 2:WW], B[:, 2:6, 0:WW - 2], ALU.add, [B_tt], [C_tt])
            self.tt_op("dve", B[:, 4:6, 7:WW], C[:, 4:6, 7:WW], C[:, 4:6, 3:WW - 4], ALU.add, [C_tt, B_tt], [B_tt])
            self.tt_op(ENG["pooladd"], B[:, 6:DC, 1:WW], A[:, 6:DC, 1:WW], A[:, 6:DC, 0:WW - 1], ALU.add, [A_tt], [B_t2[k]])
            self.tt_op(ENG["pooladd"], C[:, 6:DC, 2:WW], B[:, 6:DC, 2:WW], B[:, 6:DC, 0:WW - 2], ALU.add, [B_t2[k]], [C_t2[k]])
            self.tt_op(ENG["pooladd"], B[:, 6:DC, 7:WW], C[:, 6:DC, 7:WW], C[:, 6:DC, 3:WW - 4], ALU.add, [C_t2[k], B_t2[k]], [B_t2[k]])
            self.tt_op(ENG["pooladd"], C[:, 6:DC, 15:WW], B[:, 6:DC, 15:WW], B[:, 6:DC, 7:WW - 8], ALU.add, [B_t2[k], C_t2[k]], [C_t2[k]])
            for g, (src, stt_) in enumerate(((B, B_tt), (C, C_tt), (B, B_tt), (C, C_t2[k]))):
                w = (2, 4, 8, 16)[g]
                self.stt("dve", MX[:, 2 * g:2 * g + 2, :], src[:, 2 * g:2 * g + 2, HALO:WW], 1.0 / w, A[:, 2 * g:2 * g + 2, HALO:WW],
                         ALU.mult, ALU.subtract, [stt_, A_tt], [MX_tt])
                if tt == 0:
                    for cc in range(2):
                        c = 2 * g + cc
                        kk = self.tp_i % 2
                        self.tp_i += 1
                        self.tt_op("dve", self.tpost[kk][:, 0:16], src[:, c, HALO:HALO + 16], self.invc[:, g, :], ALU.mult,
                                   [stt_, self.t_const], [self.tpost_tt[kk]])
                        self.tt_op("dve", MX[:, c, 0:16], self.tpost[kk][:, 0:16], A[:, c, HALO:HALO + 16], ALU.subtract,
                                   [self.tpost_tt[kk], A_tt, MX_tt], [MX_tt])

        def s2(tt):
            k = tt % 2
            MX, Y, MX_tt, Y_tt = MXb[k], Yb[k], MX_t[k], Y_t[k]
            for oc in range(DC):
                g, oi = divmod(oc, 2)
                pb, pt_ = self.bank("dn", [0, 1, 2, 3, 4, 5, 6])
                for kc in range(2):
                    self.mm(pb[:, 0:TW], WP[:, g, kc, oi * 128:(oi + 1) * 128], MX[:, 2 * g + kc, :], kc == 0, kc == 1,
                            [WP_tt, MX_tt], [pt_])
                sc = self.pcol(self.G_PS + j * 8 + oc)
                self.act(Y[:, oc, :], pb[:, 0:TW], AF.Identity, [pt_, self.t_pt], [Y_tt], scale=sc)
            self.sq_rot()
            self.act(self.sq[:, :, 0:TW], Y[:, :, :], AF.Square, [Y_tt], [self.sq_tt])
            self.post_norm_add(lambda c: Y[:, c, :], [Y_tt], tt * TW, TW, self.G_MIXPOST + l * 8)

        s1(0)
        for tt in range(NT):
            if tt + 1 < NT:
                s1(tt + 1)
            s2(tt)

    def mla(self, s, l):
        P = self.P
        j = l // 2
        O_TAB, O_OT, O_CQ, O_CKV, O_KR, O_PH = 0, 16512, 49536, 61920, 70176, 74304
        Ct = self.av(O_TAB, [LT], F32, 64, 96)
        St = self.av(O_TAB + 8256, [LT], F32, 64, 96)
        tab_tt = self.alloc(O_TAB, 16512)
        self.dma("sp", Ct[:, :], self.rope_cs[0], [], [tab_tt], P.new_dma_sem('tab'))
        self.dma("sp", St[:, :], self.rope_cs[1], [], [tab_tt], P.new_dma_sem('tab'))
        a_t = [self.av(O_OT + i * 5504, [DC, TW], BF16) for i in range(2)]
        a_tt = [self.alloc(O_OT + i * 5504, 5504) for i in range(2)]
        cq = self.av(O_CQ, [3, LT], BF16)
        ckv = self.av(O_CKV, [2, LT], BF16)
        cq_tt = self.alloc(O_CQ, 12384, n=NT)
        ckv_tt = self.alloc(O_CKV, 8256, n=NT)
        krope = self.av(O_KR, [LT], BF16, 64, 96)
        kr_tt = self.alloc(O_KR, 4128)
        O_WD, O_RAW, O_RT = O_PH, O_PH + 11264, O_PH + 11264 + 6880
        Wd = self.av(O_WD, [DC, 704], BF16)
        Wd_tt = self.alloc(O_WD, 11264)
        raw = self.av(O_RAW, [5, TW], F32)
        raw_tt = self.alloc(O_RAW, 6880, n=2)
        rt = [self.av(O_RT + i * 1376, [TW], F32, 64, 96) for i in range(2)]
        rt_tt = [self.alloc(O_RT + i * 1376, 1376) for i in range(2)]
        sem = P.new_dma_sem('wdq')
        self.dma("pool", self.av(O_WD, [DC * 704], BF16)[:, :], self.w_dqkv[j], [], [Wd_tt], sem)
        P.add("dve", lambda e: e.tensor_scalar_mul(out=Wd[:, :, 672:688], in0=Wd[:, :, 672:688], scalar1=-1.0),
              reads=[Wd_tt], writes=[Wd_tt])
        self.norm_pre(0, TW, self.G_MIXPRE + l * 8, lambda c: a_t[0][:, c, :], [a_tt[0]])
        for tt in range(NT):
            t0 = tt * TW
            ab = tt % 2
            for oc in range(5):
                pb, pt_ = self.ps[oc], self.ps_tt[oc]
                for k in range(DC):
                    self.mm(pb[:, 0:TW], Wd[:, k, oc * 128:(oc + 1) * 128], a_t[ab][:, k, :], k == 0, k == DC - 1,
                            [Wd_tt, a_tt[ab]], [pt_])
            for (bk, c0) in ((5, 640), (6, 672)):
                pb, pt_ = self.ps[bk], self.ps_tt[bk]
                for k in range(DC):
                    self.mm(pb[64:96, 0:TW], Wd[:, k, c0:c0 + 32], a_t[ab][:, k, :], k == 0, k == DC - 1,
                            [Wd_tt, a_tt[ab]], [pt_], tp=(0, 64))
            if tt + 1 < NT:
                self.norm_pre(t0 + TW, TW, self.G_MIXPRE + l * 8, lambda c, ab=ab: a_t[1 - ab][:, c, :], [a_tt[1 - ab]])
            for (c0, nch, gcol, dst, dtt, ri) in ((0, 3, self.G_Q + j * 3, cq, cq_tt, 0), (3, 2, self.G_KV + j * 2, ckv, ckv_tt, 1)):
                self.sq_rot()
                for c in range(nch):
                    self.copy("act", raw[:, c0 + c, :], self.ps[c0 + c][:, 0:TW], [self.ps_tt[c0 + c]], [raw_tt[ri]])
                    self.act(self.sq[:, c, 0:TW], self.ps[c0 + c][:, 0:TW], AF.Square, [self.ps_tt[c0 + c]], [self.sq_tt])
                rstd, rtt = self.rstd_from_sq(nch, TW, 1.0 / (nch * 128))
                for c in range(nch):
                    self.stt("dve", dst[:, c, t0:t0 + TW], raw[:, c0 + c, :], self.pcol(gcol + c), rstd[:, 0:TW], ALU.mult, ALU.mult,
                             [raw_tt[ri], rtt, self.t_pt], [dtt[tt]])
            self.tt_op("dve", rt[0][:, :], self.ps[5][64:96, 0:TW], Ct[:, t0:t0 + TW], ALU.mult, [self.ps_tt[5], tab_tt], [rt_tt[0]])
            self.tt_op("dve", rt[1][:, :], self.ps[6][64:96, 0:TW], St[:, t0:t0 + TW], ALU.mult, [self.ps_tt[6], tab_tt], [rt_tt[1]])
            self.tt_op("dve", krope[:, t0:t0 + TW], rt[0][:, :], rt[1][:, :], ALU.add, [rt_tt[0], rt_tt[1]], [kr_tt])
        oT = self.av(O_OT, [DC, LT], BF16)
        oT_tt = self.alloc(O_OT, 33024, n=NT)
        o = O_PH
        kT = [self.av(o + i * 4128, [LT], BF16, 0, 96) for i in range(2)]
        kTn_tt = [self.alloc(o + i * 4128, 4128) for i in range(2)]
        kTr_tt = [TT(), TT()]
        o += 8256
        qT = [self.av(o + i * 688, [TW], BF16, 0, 96) for i in range(3)]
        qTn_tt = [self.alloc(o + i * 688, 688) for i in range(3)]
        qTr_tt = [TT(), TT(), TT()]
        o += 2064
        Va = self.av(o, [NKB, 2, 128], BF16)
        O_VA = o
        Va_tt = self.alloc(o, 8704)
        o += 8704
        Wq = [self.av(o + i * 1152, [3, 192], BF16) for i in range(2)]
        Wq_f = [self.av(o + i * 1152, [576], BF16) for i in range(2)]
        Wq_tt = [self.alloc(o + i * 1152, 1152) for i in range(2)]
        o += 2304
        Wr = [self.av(o + i * 384, [3, 2, 32], BF16) for i in range(2)]
        Wr_tt = [self.alloc(o + i * 384, 384) for i in range(2)]
        o += 768
        Wk = [self.av(o + i * 1024, [2, 256], BF16) for i in range(2)]
        Wk_f = [self.av(o + i * 1024, [512], BF16) for i in range(2)]
        Wk_tt = [self.alloc(o + i * 1024, 1024) for i in range(2)]
        o += 2048
        PTb = [self.av(o + i * 688, [TW], BF16) for i in range(4)]
        PT_tt = [self.alloc(o + i * 688, 688) for i in range(4)]
        o += 2752
        qr = [self.av(o + i * 1376, [TW], F32, 64, 96) for i in range(4)]
        qr_tt = [self.alloc(o + i * 1376, 1376) for i in range(4)]
        o += 5504
        num = [self.av(o + i * 1376, [TW], F32) for i in range(2)]
        num_tt = [self.alloc(o + i * 1376, 1376) for i in range(2)]
        o += 2752
        rden = [self.av(o + i * 1376, [TW], F32) for i in range(2)]
        rden_tt = [self.alloc(o + i * 1376, 1376) for i in range(2)]
        o += 2752
        rdb = [self.av(o + i * 1376, [TW], F32) for i in range(2)]
        rdb_tt = [self.alloc(o + i * 1376, 1376) for i in range(2)]
        rdb_sem = [P.new_dma_sem('rdb%d' % i) for i in range(2)]
        o += 2752
        wsem = [P.new_dma_sem('wh%d' % i) for i in range(2)]
        P.add("pool", lambda e: e.memset(Va[:, :, 0, 64:128], 1.0), writes=[Va_tt])
        P.add("pool", lambda e: e.memset(Va[:, :, 1, 0:64], 1.0), writes=[Va_tt])
        wsem_k = [P.new_dma_sem('whk%d' % i) for i in range(2)]

        def pair_dma(hp):
            wb = hp % 2
            self.dma("pool", Wq_f[wb][:, :], self.w_uq[j, hp], [], [Wq_tt[wb]], wsem[wb])
            self.dma("pool", Wk_f[wb][:, :], self.w_ukv[j, hp], [], [Wk_tt[wb]], wsem_k[wb])

        def wr_build(hp):
            wb = hp % 2
            wq4 = Wq[wb][:, :, :].rearrange("p k (h d) -> p k h d", h=2)
            P.add("dve", lambda e, wb=wb, wq4=wq4: e.tensor_scalar_mul(out=Wr[wb][:, :, :, 0:16], in0=wq4[:, :, :, 80:96], scalar1=-1.0),
                  reads=[Wq_tt[wb]], writes=[Wr_tt[wb]])
            self.copy("dve", Wr[wb][:, :, :, 16:32], wq4[:, :, :, 64:80], [Wq_tt[wb]], [Wr_tt[wb]])

        def v_build(hp):
            wb = hp % 2
            rhs = Wk[wb][:, :, :].rearrange("p k (h d) -> p k h d", h=2)
            Vflat = self.av(O_VA, [NKB * 256], BF16)
            for kg in range(0, NKB, 4):
                nb4 = min(4, NKB - kg)
                rows = min(128, LT - kg * 128)
                pb, pt_ = self.bank("att_s", [0, 1, 2])
                for bi in range(nb4):
                    kb = kg + bi
                    for k in range(2):
                        self.mm(pb[0:rows, bi * 128:(bi + 1) * 128], ckv[:, k, kb * 128:kb * 128 + rows], rhs[:, k, :, 64:128],
                                k == 0, k == 1, ckv_tt + [Wk_tt[wb]], [pt_])
                dst = bass.AP(Vflat.tensor, Vflat[0:rows, kg * 256:kg * 256 + 1].offset,
                              [list(Vflat[0:rows, :].ap[0]), [256, nb4], [192, 2], [1, 64]])
                src = pb[0:rows, 0:nb4 * 128].rearrange("p (b h d) -> p b h d", b=nb4, h=2)
                self.copy("dve", dst, src, [pt_], [Va_tt])

        def gen_k(hd, tt):
            wb, hi, kb_ = (hd // 2) % 2, hd % 2, hd % 2
            t0 = tt * TW
            pb, pt_ = self.ps[7], self.ps_tt[7]
            for k in range(2):
                self.mm(pb[0:64, 0:TW], Wk[wb][:, k, hi * 128:hi * 128 + 64], ckv[:, k, t0:t0 + TW], k == 0, k == 1,
                        [Wk_tt[wb], ckv_tt[tt]], [pt_])
                if k == 0:
                    yield
            self.copy("dve", kT[kb_][0:64, t0:t0 + TW], pb[0:64, 0:TW], [pt_], [kTn_tt[kb_]])
            if tt == NT - 1:
                self.copy(ENG["krope"], kT[kb_][64:96, :], krope[:, :], [kr_tt], [kTr_tt[kb_]])
            yield

        def gen_q(hd, tt, ni):
            wb, hi = (hd // 2) % 2, hd % 2
            t0 = tt * TW
            qb = ni % 3
            pq, pqt = self.ps[5], self.ps_tt[5]
            pr, prt = self.ps[6], self.ps_tt[6]
            for k in range(3):
                self.mm(pq[0:96, 0:TW], Wq[wb][:, k, hi * 96:(hi + 1) * 96], cq[:, k, t0:t0 + TW], k == 0, k == 2,
                        [Wq_tt[wb], cq_tt[tt]], [pqt])
                yield
            for k in range(3):
                self.mm(pr[64:96, 0:TW], Wr[wb][:, k, hi, :], cq[:, k, t0:t0 + TW], k == 0, k == 2,
                        [Wr_tt[wb], cq_tt[tt]], [prt], tp=(0, 64))
                if k < 2:
                    yield
            self.copy("dve", qT[qb][0:64, :], pq[0:64, 0:TW], [pqt], [qTn_tt[qb]])
            rb = ni % 2
            r0, r1 = qr[rb * 2], qr[rb * 2 + 1]
            r0t, r1t = qr_tt[rb * 2], qr_tt[rb * 2 + 1]
            self.tt_op("dve", r0[:, :], pq[64:96, 0:TW], Ct[:, t0:t0 + TW], ALU.mult, [pqt, tab_tt], [r0t])
            self.tt_op("dve", r1[:, :], pr[64:96, 0:TW], St[:, t0:t0 + TW], ALU.mult, [prt, tab_tt], [r1t])
            self.tt_op(ENG["ropeadd"], qT[qb][64:96, :], r0[:, :], r1[:, :], ALU.add, [r0t, r1t], [qTr_tt[qb]])
            yield

        def drain(g):
            for _ in g:
                pass

        def emit_att(hd, tt, ni, filler):
            hp, hi, kb_ = hd // 2, hd % 2, hd % 2
            t0 = tt * TW
            qb = ni % 3
            po, pot = self.bank("att_o", [3, 4])
            q_end = t0 + TW
            nkb = (q_end - 1) // 128 + 1
            info = {}

            def s_emit(kb):
                k0 = kb * 128
                rows = min(128, LT - k0)
                c_lo = max(0, k0 - t0)
                pb, pt_ = self.bank("att_s", [0, 1, 2])
                full = (k0 + rows - 1) <= t0 + c_lo
                self.mm(pb[0:rows, c_lo:TW], kT[kb_][0:96, k0:k0 + rows], qT[qb][0:96, c_lo:TW], True, full,
                        [kTn_tt[kb_], kTr_tt[kb_], qTn_tt[qb], qTr_tt[qb]], [pt_])
                if not full:
                    mc_hi = min(k0 + 128, q_end) - t0
                    m0 = t0 + c_lo - k0
                    m1 = t0 + mc_hi - k0
                    self.mm(pb[0:rows, c_lo:mc_hi], self.identb[0:rows, 0:rows], self.maskb[0:rows, m0:m1], False, True,
                            [self.t_const], [pt_])
                pk = self.rr.get("ptb", 0) % 4
                self.rr["ptb"] = self.rr.get("ptb", 0) + 1
                self.act(PTb[pk][0:rows, c_lo:TW], pb[0:rows, c_lo:TW], AF.Exp, [pt_], [PT_tt[pk]], scale=SCALE)
                info[kb] = (rows, c_lo, pk)

            def pv_emit(kb):
                rows, c_lo, pk = info[kb]
                self.mm(po[:, c_lo:TW], Va[0:rows, kb, hi, :], PTb[pk][0:rows, c_lo:TW], kb == 0, kb == nkb - 1,
                        [Va_tt, PT_tt[pk]], [pot])

            s_emit(0)
            if nkb > 1:
                s_emit(1)
            for kb in range(nkb):
                if kb + 2 < nkb:
                    s_emit(kb + 2)
                pv_emit(kb)
                next(filler, None)
            drain(filler)
            nb = ni % 2
            n0, d0 = (0, 64) if hi == 0 else (64, 0)
            P.add("dve", lambda e, nb=nb, d0=d0, po=po: e.reciprocal(out=rden[nb][d0:d0 + 64, :], in_=po[d0:d0 + 64, 0:TW]),
                  reads=[pot], writes=[rden_tt[nb]])
            self.copy("dve", num[nb][n0:n0 + 64, :], po[n0:n0 + 64, 0:TW], [pot], [num_tt[nb]])
            self.dma("sp", rdb[nb][n0:n0 + 64, :], rden[nb][d0:d0 + 64, :], [rden_tt[nb]], [rdb_tt[nb]], rdb_sem[nb])
            self.tt_op(ENG["otmul"], oT[n0:n0 + 64, hp, t0:t0 + TW], num[nb][n0:n0 + 64, :], rdb[nb][n0:n0 + 64, :], ALU.mult,
                       [num_tt[nb], rdb_tt[nb]], [oT_tt[tt]])

        import itertools
        wo_box = []
        units = [(hd, tt) for hd in range(NH) for tt in range(NT)]
        pair_dma(0)
        wr_build(0)
        v_build(0)
        for t2 in range(NT):
            drain(gen_k(0, t2))
        drain(gen_q(0, 0, 0))
        drain(gen_q(0, 1, 1))
        for ui, (hd, tt) in enumerate(units):
            hp, hi = hd // 2, hd % 2
            if hi == 0 and tt == 0:
                if hp + 1 < NH // 2:
                    pair_dma(hp + 1)
                if hp > 0:
                    v_build(hp)
            if hi == 1 and tt == 0 and hp + 1 < NH // 2:
                wr_build(hp + 1)
            fill = []
            if ui + 2 < len(units):
                nhd, ntt = units[ui + 2]
                fill.append(gen_q(nhd, ntt, ui + 2))
            if hd + 1 < NH:
                fill.append(gen_k(hd + 1, tt))
            emit_att(hd, tt, ui, itertools.chain(*fill))
            if ui == len(units) - 3:
                wo_box.append(self.alloc(O_CQ, 16384))
                self.dma("pool", self.av(O_CQ, [DC * D], BF16)[:, :], self.w_o[j], [], [wo_box[0]], P.new_dma_sem('wo'))
        Wo = self.av(O_CQ, [DC, D], BF16)
        Wo_tt = wo_box[0]
        mrs = [self.av(O_PH + i * 11008, [DC, TW], F32) for i in range(2)]
        mrs_tt = [self.alloc(O_PH + i * 11008, 11008) for i in range(2)]

        def wo_mm(tt):
            t0 = tt * TW
            mr, mr_tt = mrs[tt % 2], mrs_tt[tt % 2]
            for oc in range(DC):
                pb, pt_ = self.bank("dn", [0, 1, 2, 3, 4, 5, 6])
                for kc in range(DC):
                    self.mm(pb[:, 0:TW], Wo[:, kc, oc * 128:(oc + 1) * 128], oT[:, kc, t0:t0 + TW], kc == 0, kc == DC - 1,
                            [Wo_tt, oT_tt[tt]], [pt_])
                self.copy("act", mr[:, oc, :], pb[:, 0:TW], [pt_], [mr_tt])
            self.sq_rot()
            self.act(self.sq[:, :, 0:TW], mr[:, :, :], AF.Square, [mr_tt], [self.sq_tt])
            sq_of[tt] = (self.sq, self.sq_tt)

        sq_of = {}

        def wo_pn(tt):
            mr, mr_tt = mrs[tt % 2], mrs_tt[tt % 2]
            cur = (self.sq, self.sq_tt)
            self.sq, self.sq_tt = sq_of[tt]
            self.post_norm_add(lambda c: mr[:, c, :], [mr_tt], tt * TW, TW, self.G_MIXPOST + l * 8)
            self.sq, self.sq_tt = cur

        wo_mm(0)
        for tt in range(NT):
            if tt + 1 < NT:
                wo_mm(tt + 1)
            wo_pn(tt)

    def build(self):
        self.setup()
        nseq = self.cfg.get("nseq", 2)
        for s in range(nseq):
            if s == 0:
                self.seq_io(None, 0)
            for (l, do_mix, do_ffn) in self.cfg["layers"]:
                if do_mix:
                    if l % 2 == 0:
                        self.mla(s, l)
                    else:
                        self.pool_mixer(s, l)
                if do_ffn:
                    self.ffn(s, l)
            self.seq_io(s, s + 1 if s + 1 < nseq else None)
        self.P.finalize(self.st)


FULL_CFG = {"layers": [(0, True, True), (1, True, True), (2, True, True), (3, True, True)], "nseq": 2}


def build_nc(cfg=None):
    cfg = cfg or FULL_CFG
    nc = bass.Bass("TRN2", target_bir_lowering=False)
    st = ExitStack()
    with st:
        mk = MK(nc, cfg, st)
        mk.build()
    return nc


def rope_consts():
    inv = (1.0 / (np.float32(10000.0) ** (np.arange(0, RR, 2, dtype=np.float32) / np.float32(RR)))).astype(np.float32)
    ang = (np.arange(LT, dtype=np.float32)[None, :] * inv[:, None]).astype(np.float32)
    c, s = np.cos(ang).astype(np.float32), np.sin(ang).astype(np.float32)
    return np.ascontiguousarray(np.stack([np.concatenate([c, c], 0), np.concatenate([s, s], 0)], 0))


def relayout(w):
    c = np.ascontiguousarray
    o = {}
    wd = w["mla_w_dqkv"].reshape(2, DC, 128, 672)
    wd = np.concatenate([wd, wd[..., 656:672], wd[..., 640:656]], axis=-1)
    o["mla_w_dqkv"] = c(wd.transpose(0, 2, 1, 3)).reshape(2, 128, DC * 704)
    o["mla_w_uq"] = c(w["mla_w_uq"].reshape(2, 3, 128, 8, 192).transpose(0, 3, 2, 1, 4)).reshape(2, 8, 128, 576)
    o["mla_w_ukv"] = c(w["mla_w_ukv"].reshape(2, 2, 128, 8, 256).transpose(0, 3, 2, 1, 4)).reshape(2, 8, 128, 512)
    o["mla_w_o"] = c(w["mla_w_o"].reshape(2, DC, 128, D).transpose(0, 2, 1, 3)).reshape(2, 128, DC * D)
    o["pool_w_group"] = c(w["pool_w_group"].reshape(2, 4, 2, 128, 256).transpose(0, 3, 1, 2, 4)).reshape(2, 128, 2048)
    o["ffn_w_up"] = c(w["ffn_w_up"].reshape(4, DC, 128, 2, FC // 2, 256).transpose(0, 4, 2, 3, 1, 5)).reshape(4, FC // 2, 128, 4096)
    o["ffn_w_down"] = c(w["ffn_w_down"].reshape(4, FC, 128, 4, 256).transpose(0, 3, 2, 1, 4)).reshape(4, 4, 128, FC * 256)
    return o


_NC_CACHE = {}


def run(inputs, cfg=None, trace=False, ncores=NCORES):
    key = repr(cfg)
    if key not in _NC_CACHE:
        _NC_CACHE[key] = build_nc(cfg)
    nc = _NC_CACHE[key]
    x = np.ascontiguousarray(np.asarray(inputs["x"], dtype=np.float32))
    shared = {k: np.ascontiguousarray(np.asarray(v, dtype=np.float32)) for k, v in inputs.items() if k != "x"}
    shared.update(relayout(shared))
    shared["rope_cs"] = rope_consts()
    in_maps = []
    for c in range(ncores):
        m = dict(shared)
        m["x"] = np.ascontiguousarray(x[2 * c:2 * c + 2])
        in_maps.append(m)
    res = run_bass_kernel_spmd(nc, in_maps, core_ids=list(range(ncores)), **({"trace": True} if trace else {}))
    out = np.concatenate([r["out"] for r in res.results], axis=0)
    return out.astype(np.float32), res


def kernel(**inputs):
    out, _ = run(inputs)
    return out
```
